# Optimizing a Trainium2 kernel written in Bass

```python
import math
import jax
import jax.numpy as jnp
from jax import lax
import numpy as np


D_MODEL = 2048
BATCH = 2
SEQ = 4096
DEPTH = 4

N_MIXERS = 4
N_SSD = (DEPTH + 3) // 4
N_FOX = (DEPTH + 2) // 4
N_MLA = (DEPTH + 1) // 4
N_S5 = DEPTH // 4
ALPHA = (2.0 * DEPTH) ** 0.25
BETA = (8.0 * DEPTH) ** -0.25
LN_EPS = 1e-5
RMS_EPS = 1e-6
Q_BLOCK = 128

SSD_EXPAND = 2
SSD_D_INNER = SSD_EXPAND * D_MODEL
SSD_HEADDIM = 64
SSD_HEADS = SSD_D_INNER // SSD_HEADDIM
SSD_GROUPS = 8
SSD_STATE = 128
SSD_CONV = 4
SSD_CHUNK = 128
SSD_CONV_DIM = SSD_D_INNER + 2 * SSD_GROUPS * SSD_STATE
SSD_IN_DIM = SSD_D_INNER + SSD_CONV_DIM + SSD_HEADS
DT_PROJ_SCALE = 0.1

FOX_HEADS = 16
FOX_HEAD_DIM = D_MODEL // FOX_HEADS
FOX_WIDTH = FOX_HEADS * FOX_HEAD_DIM
FOX_IN_DIM = 4 * FOX_WIDTH + FOX_HEADS
FGATE_PROJ_SCALE = 0.1

MLA_HEADS = 16
MLA_Q_RANK = 512
MLA_KV_RANK = 512
MLA_NOPE = 128
MLA_ROPE = 64
MLA_V = 128
MLA_WIDTH = MLA_HEADS * MLA_V
MLA_IN_DIM = MLA_Q_RANK + MLA_KV_RANK + MLA_ROPE + MLA_WIDTH
ROPE_BASE = 10000.0

S5_WIDTH = D_MODEL
S5_GROUP = 16
S5_GROUPS = S5_WIDTH // S5_GROUP
S5_STATE = 64
S5_IN_DIM = 2 * S5_WIDTH

kernel_name = 'hybrid_ssd_fox_mla_s5_deepnorm'


def layer_norm(x, g, b):
    xf = x.astype(jnp.float32)
    mu = jnp.mean(xf, -1, keepdims=True)
    var = jnp.mean(jnp.square(xf - mu), -1, keepdims=True)
    return ((xf - mu) * lax.rsqrt(var + LN_EPS) * g.astype(jnp.float32) + b.astype(jnp.float32)).astype(x.dtype)


def rms_norm(x, w):
    xf = x.astype(jnp.float32)
    y = xf * lax.rsqrt(jnp.mean(jnp.square(xf), -1, keepdims=True) + RMS_EPS)
    return (y * w.astype(jnp.float32)).astype(x.dtype)


def causal_depthwise_conv(x, w, bias):
    k = w.shape[0]
    y = lax.conv_general_dilated(x, w[:, None, :].astype(x.dtype), window_strides=(1,), padding=[(k - 1, 0)], dimension_numbers=('NWC', 'WIO', 'NWC'), feature_group_count=x.shape[-1])
    return y + bias.astype(x.dtype)


def rope_cos_sin(positions, dim):
    inv_freq = ROPE_BASE ** (-jnp.arange(0, dim, 2, dtype=jnp.float32) / dim)
    ang = positions.astype(jnp.float32)[..., None] * inv_freq
    return jnp.cos(ang), jnp.sin(ang)


def apply_rope(x, cos, sin):
    xf = x.astype(jnp.float32)
    x1, x2 = jnp.split(xf, 2, axis=-1)
    return jnp.concatenate([x1 * cos - x2 * sin, x1 * sin + x2 * cos], axis=-1).astype(x.dtype)


def causal_block_attention(q, k, v, scale, cum=None):
    bsz, T, H, dq = q.shape
    nblk = T // Q_BLOCK
    key_pos = jnp.arange(T)
    q_blocks = jnp.moveaxis(q.reshape(bsz, nblk, Q_BLOCK, H, dq), 1, 0)
    starts = jnp.arange(nblk) * Q_BLOCK
    if cum is None:
        xs = (q_blocks, starts)
    else:
        cum = cum.astype(jnp.float32)
        cum_k = jnp.swapaxes(cum, 1, 2)
        xs = (q_blocks, starts, jnp.moveaxis(cum.reshape(bsz, nblk, Q_BLOCK, H), 1, 0))

    def attend(blk):
        qb, s0 = blk[0], blk[1]
        logits = jnp.einsum('bqhd,bkhd->bhqk', qb, k, preferred_element_type=jnp.float32) * scale
        if cum is not None:
            cq = jnp.swapaxes(blk[2], 1, 2)
            logits = logits + cq[..., :, None] - cum_k[:, :, None, :]
        query_pos = s0 + jnp.arange(Q_BLOCK)
        logits = jnp.where(key_pos[None, :] <= query_pos[:, None], logits, -jnp.inf)
        probs = jax.nn.softmax(logits, axis=-1)
        return jnp.einsum('bhqk,bkhd->bqhd', probs.astype(v.dtype), v)

    out = lax.map(attend, xs)
    return jnp.moveaxis(out, 0, 1).reshape(bsz, T, H, v.shape[-1])


def ssd_mixer(h, w_in, conv_w, conv_b, dt_bias, a_log, d_skip, norm_w, w_out):
    bsz, T, _ = h.shape
    G, E, P, N, Q = SSD_GROUPS, SSD_HEADS // SSD_GROUPS, SSD_HEADDIM, SSD_STATE, SSD_CHUNK
    nc = T // Q
    f32 = jnp.float32
    z, xbc, dt = jnp.split(h @ w_in, [SSD_D_INNER, SSD_D_INNER + SSD_CONV_DIM], axis=-1)
    xbc = jax.nn.silu(causal_depthwise_conv(xbc, conv_w, conv_b)).astype(f32)
    xs, bm, cm = jnp.split(xbc, [SSD_D_INNER, SSD_D_INNER + G * N], axis=-1)
    x = xs.reshape(bsz, nc, Q, G, E, P)
    bm = bm.reshape(bsz, nc, Q, G, N)
    cm = cm.reshape(bsz, nc, Q, G, N)
    dt = jax.nn.softplus(dt.astype(f32) + dt_bias.astype(f32)).reshape(bsz, nc, Q, G, E)
    a = -jnp.exp(a_log.astype(f32)).reshape(G, E)
    xdt = x * dt[..., None]
    acs = jnp.moveaxis(jnp.cumsum(dt * a, axis=2), 2, -1)
    idx = jnp.arange(Q)
    seg = acs[..., :, None] - acs[..., None, :]
    decay = jnp.exp(jnp.where(idx[:, None] >= idx[None, :], seg, -jnp.inf))
    cb = jnp.einsum('bclgn,bcsgn->bcgls', cm, bm)
    y_diag = jnp.einsum('bcgls,bcgels,bcsgep->bclgep', cb, decay, xdt)
    decay_to_end = jnp.exp(acs[..., -1:] - acs)
    states = jnp.einsum('bclgn,bcgel,bclgep->bcgepn', bm, decay_to_end, xdt)
    chunk_decay = jnp.exp(acs[..., -1])

    def carry_state(s, inp):
        st, dec = inp
        return s * dec[..., None, None] + st, s

    _, prev = lax.scan(carry_state, jnp.zeros_like(states[:, 0]), (jnp.moveaxis(states, 1, 0), jnp.moveaxis(chunk_decay, 1, 0)))
    prev = jnp.moveaxis(prev, 0, 1)
    y_off = jnp.einsum('bclgn,bcgepn,bcgel->bclgep', cm, prev, jnp.exp(acs))
    y = y_diag + y_off + d_skip.astype(f32).reshape(G, E, 1) * x
    y = y.reshape(bsz, T, SSD_D_INNER) * jax.nn.silu(z.astype(f32))
    yg = y.reshape(bsz, T, G, SSD_D_INNER // G)
    yg = yg * lax.rsqrt(jnp.mean(jnp.square(yg), -1, keepdims=True) + RMS_EPS)
    y = yg.reshape(bsz, T, SSD_D_INNER) * norm_w.astype(f32)
    return y.astype(h.dtype) @ w_out


def fox_mixer(h, w_in, f_bias, w_out):
    bsz, T, _ = h.shape
    W = FOX_WIDTH
    q, k, v, z, f = jnp.split(h @ w_in, [W, 2 * W, 3 * W, 4 * W], axis=-1)
    shp = (bsz, T, FOX_HEADS, FOX_HEAD_DIM)
    log_f = jax.nn.log_sigmoid(f.astype(jnp.float32) + f_bias.astype(jnp.float32))
    cum = jnp.cumsum(log_f, axis=1)
    o = causal_block_attention(q.reshape(shp), k.reshape(shp), v.reshape(shp), FOX_HEAD_DIM ** -0.5, cum)
    y = o.reshape(bsz, T, W) * jax.nn.silu(z)
    return y @ w_out


def mla_mixer(h, positions, w_in, q_norm, kv_norm, w_q_up, w_kv_up, w_out):
    bsz, T, _ = h.shape
    H = MLA_HEADS
    q_lat, kv_lat, k_pe, z = jnp.split(h @ w_in, [MLA_Q_RANK, MLA_Q_RANK + MLA_KV_RANK, MLA_Q_RANK + MLA_KV_RANK + MLA_ROPE], axis=-1)
    q = (rms_norm(q_lat, q_norm) @ w_q_up).reshape(bsz, T, H, MLA_NOPE + MLA_ROPE)
    kv = (rms_norm(kv_lat, kv_norm) @ w_kv_up).reshape(bsz, T, H, MLA_NOPE + MLA_V)
    cos, sin = rope_cos_sin(positions, MLA_ROPE)
    q_pe = apply_rope(q[..., MLA_NOPE:], cos[:, :, None, :], sin[:, :, None, :])
    k_pe = apply_rope(k_pe, cos, sin)
    q = jnp.concatenate([q[..., :MLA_NOPE], q_pe], axis=-1)
    k = jnp.concatenate([kv[..., :MLA_NOPE], jnp.broadcast_to(k_pe[:, :, None, :], (bsz, T, H, MLA_ROPE))], axis=-1)
    o = causal_block_attention(q, k, kv[..., MLA_NOPE:], (MLA_NOPE + MLA_ROPE) ** -0.5)
    y = o.reshape(bsz, T, MLA_WIDTH) * jax.nn.silu(z)
    return y @ w_out


def _diag_linear_combine(left, right):
    a_l, b_l = left
    a_r, b_r = right
    return a_r * a_l, a_r * b_l + b_r


def s5_mixer(h, w_in, lam_re, lam_im, log_step, b_re, b_im, c_re, c_im, d_skip, w_glu, b_glu, w_out):
    bsz, T, _ = h.shape
    f32 = jnp.float32
    u, z = jnp.split(h @ w_in, 2, axis=-1)
    u = u.astype(f32)
    lam = lax.complex(lam_re.astype(f32), lam_im.astype(f32))
    step = jnp.exp(log_step.astype(f32))[:, None]
    lam_bar = jnp.exp(lam * step)
    b_bar = ((lam_bar - 1.0) / lam)[..., None] * lax.complex(b_re.astype(f32), b_im.astype(f32))
    ug = u.reshape(bsz, T, S5_GROUPS, S5_GROUP).astype(jnp.complex64)
    bu = jnp.einsum('gpi,btgi->btgp', b_bar, ug)
    a_seq = jnp.broadcast_to(lam_bar, (1, T, S5_GROUPS, S5_STATE))
    _, states = lax.associative_scan(_diag_linear_combine, (a_seq, bu), axis=1)
    c = lax.complex(c_re.astype(f32), c_im.astype(f32))
    y = jnp.einsum('gip,btgp->btgi', c, states).real.reshape(bsz, T, S5_WIDTH) + d_skip.astype(f32) * u
    y = jax.nn.gelu(y)
    y = y * jax.nn.sigmoid(y @ w_glu.astype(f32) + b_glu.astype(f32))
    y = y * jax.nn.silu(z.astype(f32))
    return y.astype(h.dtype) @ w_out


def _normal(key, shape, std):
    return jax.random.normal(key, shape, jnp.float32) * std


def setup_inputs(seed: int = 0) -> dict:
    key = jax.random.key(seed)
    keys = iter(jax.random.split(key, 48))
    D = D_MODEL
    x = _normal(next(keys), (BATCH, SEQ, D), 1.0)
    positions = (jax.random.randint(next(keys), (BATCH, 1), 0, 1024) + jnp.arange(SEQ)[None, :]).astype(jnp.int32)
    ln_g = 1.0 + _normal(next(keys), (DEPTH, D), 0.02)
    ln_b = _normal(next(keys), (DEPTH, D), 0.02)
    L = N_SSD
    ssd_cols = jnp.concatenate([jnp.ones((SSD_IN_DIM - SSD_HEADS,), jnp.float32), jnp.full((SSD_HEADS,), DT_PROJ_SCALE, jnp.float32)])
    ssd_w_in = _normal(next(keys), (L, D, SSD_IN_DIM), D ** -0.5) * ssd_cols
    ssd_conv_w = _normal(next(keys), (L, SSD_CONV, SSD_CONV_DIM), SSD_CONV ** -0.5)
    ssd_conv_b = _normal(next(keys), (L, SSD_CONV_DIM), 0.02)
    dt0 = jnp.exp(jax.random.uniform(next(keys), (L, SSD_HEADS), jnp.float32, math.log(1e-3), math.log(1e-1)))
    ssd_dt_bias = dt0 + jnp.log(-jnp.expm1(-dt0))
    ssd_a_log = jnp.log(jax.random.uniform(next(keys), (L, SSD_HEADS), jnp.float32, 1.0, 16.0))
    ssd_d = 1.0 + _normal(next(keys), (L, SSD_HEADS), 0.1)
    ssd_norm_w = 1.0 + _normal(next(keys), (L, SSD_D_INNER), 0.02)
    ssd_w_out = _normal(next(keys), (L, SSD_D_INNER, D), SSD_D_INNER ** -0.5 * BETA)
    L = N_FOX
    fox_cols = jnp.concatenate([jnp.ones((4 * FOX_WIDTH,), jnp.float32), jnp.full((FOX_HEADS,), FGATE_PROJ_SCALE, jnp.float32)])
    fox_w_in = _normal(next(keys), (L, D, FOX_IN_DIM), D ** -0.5) * fox_cols
    fox_f_bias = jax.random.uniform(next(keys), (L, FOX_HEADS), jnp.float32, 1.0, 6.0)
    fox_w_out = _normal(next(keys), (L, FOX_WIDTH, D), FOX_WIDTH ** -0.5 * BETA)
    L = N_MLA
    mla_w_in = _normal(next(keys), (L, D, MLA_IN_DIM), D ** -0.5)
    mla_q_norm = 1.0 + _normal(next(keys), (L, MLA_Q_RANK), 0.02)
    mla_kv_norm = 1.0 + _normal(next(keys), (L, MLA_KV_RANK), 0.02)
    mla_w_q_up = _normal(next(keys), (L, MLA_Q_RANK, MLA_HEADS * (MLA_NOPE + MLA_ROPE)), MLA_Q_RANK ** -0.5)
    mla_w_kv_up = _normal(next(keys), (L, MLA_KV_RANK, MLA_HEADS * (MLA_NOPE + MLA_V)), MLA_KV_RANK ** -0.5)
    mla_w_out = _normal(next(keys), (L, MLA_WIDTH, D), MLA_WIDTH ** -0.5 * BETA)
    L = N_S5
    s5_w_in = _normal(next(keys), (L, D, S5_IN_DIM), D ** -0.5)
    s5_lambda_re = -0.5 + _normal(next(keys), (L, S5_GROUPS, S5_STATE), 0.01)
    s5_lambda_im = math.pi * jnp.arange(S5_STATE, dtype=jnp.float32) + _normal(next(keys), (L, S5_GROUPS, S5_STATE), 0.01)
    s5_log_step = jax.random.uniform(next(keys), (L, S5_GROUPS), jnp.float32, math.log(1e-3), math.log(1e-1))
    s5_b_re = _normal(next(keys), (L, S5_GROUPS, S5_STATE, S5_GROUP), (2.0 * S5_GROUP) ** -0.5)
    s5_b_im = _normal(next(keys), (L, S5_GROUPS, S5_STATE, S5_GROUP), (2.0 * S5_GROUP) ** -0.5)
    s5_c_re = _normal(next(keys), (L, S5_GROUPS, S5_GROUP, S5_STATE), S5_STATE ** -0.5)
    s5_c_im = _normal(next(keys), (L, S5_GROUPS, S5_GROUP, S5_STATE), S5_STATE ** -0.5)
    s5_d = 1.0 + _normal(next(keys), (L, S5_WIDTH), 0.1)
    s5_w_glu = _normal(next(keys), (L, S5_WIDTH, S5_WIDTH), S5_WIDTH ** -0.5)
    s5_b_glu = _normal(next(keys), (L, S5_WIDTH), 0.02)
    s5_w_out = _normal(next(keys), (L, S5_WIDTH, D), S5_WIDTH ** -0.5 * BETA)
    return {'x': x, 'positions': positions, 'ln_g': ln_g, 'ln_b': ln_b,
            'ssd_w_in': ssd_w_in, 'ssd_conv_w': ssd_conv_w, 'ssd_conv_b': ssd_conv_b, 'ssd_dt_bias': ssd_dt_bias,
            'ssd_a_log': ssd_a_log, 'ssd_d': ssd_d, 'ssd_norm_w': ssd_norm_w, 'ssd_w_out': ssd_w_out,
            'fox_w_in': fox_w_in, 'fox_f_bias': fox_f_bias, 'fox_w_out': fox_w_out,
            'mla_w_in': mla_w_in, 'mla_q_norm': mla_q_norm, 'mla_kv_norm': mla_kv_norm, 'mla_w_q_up': mla_w_q_up,
            'mla_w_kv_up': mla_w_kv_up, 'mla_w_out': mla_w_out,
            's5_w_in': s5_w_in, 's5_lambda_re': s5_lambda_re, 's5_lambda_im': s5_lambda_im, 's5_log_step': s5_log_step,
            's5_b_re': s5_b_re, 's5_b_im': s5_b_im, 's5_c_re': s5_c_re, 's5_c_im': s5_c_im, 's5_d': s5_d,
            's5_w_glu': s5_w_glu, 's5_b_glu': s5_b_glu, 's5_w_out': s5_w_out}


def reference(x, positions, ln_g, ln_b,
              ssd_w_in, ssd_conv_w, ssd_conv_b, ssd_dt_bias, ssd_a_log, ssd_d, ssd_norm_w, ssd_w_out,
              fox_w_in, fox_f_bias, fox_w_out,
              mla_w_in, mla_q_norm, mla_kv_norm, mla_w_q_up, mla_w_kv_up, mla_w_out,
              s5_w_in, s5_lambda_re, s5_lambda_im, s5_log_step, s5_b_re, s5_b_im, s5_c_re, s5_c_im, s5_d,
              s5_w_glu, s5_b_glu, s5_w_out):
    h = x
    for i in range(DEPTH):
        j = i // N_MIXERS
        kind = i % N_MIXERS
        if kind == 0:
            out = ssd_mixer(h, ssd_w_in[j], ssd_conv_w[j], ssd_conv_b[j], ssd_dt_bias[j], ssd_a_log[j], ssd_d[j], ssd_norm_w[j], ssd_w_out[j])
        elif kind == 1:
            out = fox_mixer(h, fox_w_in[j], fox_f_bias[j], fox_w_out[j])
        elif kind == 2:
            out = mla_mixer(h, positions, mla_w_in[j], mla_q_norm[j], mla_kv_norm[j], mla_w_q_up[j], mla_w_kv_up[j], mla_w_out[j])
        else:
            out = s5_mixer(h, s5_w_in[j], s5_lambda_re[j], s5_lambda_im[j], s5_log_step[j], s5_b_re[j], s5_b_im[j], s5_c_re[j], s5_c_im[j], s5_d[j], s5_w_glu[j], s5_b_glu[j], s5_w_out[j])
        h = layer_norm(ALPHA * h + out.astype(h.dtype), ln_g[i], ln_b[i])
    return h
```

```python
import contextlib
import os
import numpy as np
import ml_dtypes
import concourse.bass as bass
import concourse.mybir as mybir
from concourse.bass_utils import run_bass_kernel_spmd

F32 = mybir.dt.float32
BF16 = mybir.dt.bfloat16
I32 = mybir.dt.int32
AF = mybir.ActivationFunctionType
ALU = mybir.AluOpType
AX = mybir.AxisListType

ENGS = ("pe", "act", "dve", "pool", "sp")

D_MODEL = 2048
SEQ = 4096
NB = 2
NCORES = 8
GRP = 4
TOK_OWN = SEQ // GRP
ALPHA = (2.0 * 4) ** 0.25
LN_EPS = 1e-5
RMS_EPS = 1e-6
NEG = -30000.0


class Buf:
    __slots__ = ("name", "last_writer", "readers", "dma_sem", "dma_cnt", "excl")

    def __init__(self, name):
        self.name = name
        self.excl = False
        self.last_writer = None
        self.readers = []
        self.dma_sem = None
        self.dma_cnt = 0


class Op:
    __slots__ = ("eng", "fn", "is_dma", "deps", "signaled", "rank", "dma_buf", "dma_val", "idx")

    def __init__(self, eng, fn, is_dma):
        self.eng = eng
        self.fn = fn
        self.is_dma = is_dma
        self.deps = []
        self.signaled = False
        self.rank = 0
        self.dma_buf = None
        self.dma_val = 0


class Prog:
    def __init__(self, nc):
        self.nc = nc
        self.ops = []
        self.bufs = []

    def buf(self, name="b"):
        b = Buf(name)
        self.bufs.append(b)
        return b

    def _add(self, eng, fn, reads, writes, is_dma=False):
        op = Op(eng, fn, is_dma)
        op.idx = len(self.ops)
        deps = {}
        for b in reads:
            w = b.last_writer
            if w is not None:
                deps[id(w)] = (w, True)
            if b.excl:
                for r in b.readers:
                    if id(r) not in deps:
                        deps[id(r)] = (r, False)
        for b in writes:
            w = b.last_writer
            if w is not None and id(w) not in deps:
                deps[id(w)] = (w, False)
            for r in b.readers:
                if id(r) not in deps:
                    deps[id(r)] = (r, False)
        for b in reads:
            b.readers.append(op)
        for b in writes:
            b.last_writer = op
            b.readers = []
        if is_dma:
            tb = writes[0]
            op.dma_buf = tb
            tb.dma_cnt += 1
            op.dma_val = 16 * tb.dma_cnt
        for d, raw in deps.values():
            if d is op:
                continue
            if d.is_dma and is_dma and not raw:
                if all(d is not r for b in writes for r in b.readers):
                    continue
            if (not d.is_dma) and (not is_dma) and d.eng == eng:
                if eng == "pe":
                    continue
            op.deps.append(d)
        self.ops.append(op)
        return op

    def pe(self, fn, reads, writes):
        return self._add("pe", fn, reads, writes)

    def act(self, fn, reads, writes):
        return self._add("act", fn, reads, writes)

    def dve(self, fn, reads, writes):
        return self._add("dve", fn, reads, writes)

    def pool(self, fn, reads, writes):
        return self._add("pool", fn, reads, writes)

    def dma(self, q, fn, reads, writes):
        return self._add(q, fn, reads, writes, is_dma=True)

    def check_deadlock(self):
        per_eng = {e: [o for o in self.ops if o.eng == e] for e in ENGS}
        ptr = {e: 0 for e in ENGS}
        done = set()
        total = len(self.ops)
        while len(done) < total:
            prog = False
            for e in ENGS:
                while ptr[e] < len(per_eng[e]):
                    op = per_eng[e][ptr[e]]
                    if all(id(d) in done for d in op.deps):
                        done.add(id(op))
                        ptr[e] += 1
                        prog = True
                    else:
                        break
            if not prog:
                for e in ENGS:
                    if ptr[e] < len(per_eng[e]):
                        op = per_eng[e][ptr[e]]
                        print("STUCK", e, op.idx, [(d.eng, d.idx) for d in op.deps if id(d) not in done])
                raise RuntimeError("deadlock in recorded program")

    def emit(self, final_wait_bufs=()):
        nc = self.nc
        self.check_deadlock()
        fin = Op("sp", None, False)
        for b in final_wait_bufs:
            if b.last_writer is not None:
                fin.deps.append(b.last_writer)
        last = {}
        for op in self.ops:
            last[op.eng] = op
        for e, op in last.items():
            if e != "sp" and op not in fin.deps:
                fin.deps.append(op)
        ops = self.ops + [fin]
        for op in ops:
            for d in op.deps:
                d.signaled = True
        cnt = {e: 0 for e in ENGS}
        for op in ops:
            if op.is_dma or op.fn is None:
                continue
            if op.signaled:
                cnt[op.eng] += 1
                op.rank = cnt[op.eng]
        per_eng = {e: [o for o in ops if o.eng == e] for e in ENGS}
        self.stats = {e: len(per_eng[e]) for e in ENGS}
        with contextlib.ExitStack() as stack:
            esem = {e: stack.enter_context(nc.semaphore("s_" + e)) for e in ENGS}
            nd = 0
            for b in self.bufs:
                if b.dma_cnt > 0:
                    b.dma_sem = stack.enter_context(nc.semaphore("d%d" % nd))
                    nd += 1
            self.stats["dma_sems"] = nd
            block = stack.enter_context(nc.Block())

            def run(engname, eng):
                known = {}
                for op in per_eng[engname]:
                    need = {}
                    for d in op.deps:
                        if d.is_dma:
                            s, v = d.dma_buf.dma_sem, d.dma_val
                        else:
                            s, v = esem[d.eng], d.rank
                        k = id(s)
                        if known.get(k, 0) >= v:
                            continue
                        if k not in need or need[k][1] < v:
                            need[k] = (s, v)
                    for k, (s, v) in need.items():
                        eng.wait_ge(s, v)
                        known[k] = v
                    if op.fn is None:
                        continue
                    ins = op.fn(eng)
                    if op.is_dma:
                        ins.then_inc(op.dma_buf.dma_sem, 16)
                    elif op.signaled:
                        ins.then_inc(esem[engname], 1)

            @block.tensor
            def _(e):
                run("pe", e)

            @block.scalar
            def _(e):
                run("act", e)

            @block.vector
            def _(e):
                run("dve", e)

            @block.gpsimd
            def _(e):
                run("pool", e)

            @block.sync
            def _(e):
                run("sp", e)


class Ctx:
    def __init__(self, nc, stack):
        self.nc = nc
        self.stack = stack
        self.P = Prog(nc)
        self._n = 0

    def sb(self, shape, dtype, name=None):
        self._n += 1
        t = self.stack.enter_context(self.nc.sbuf_tensor("%s_%d" % (name or "sb", self._n), list(shape), dtype))
        return t

    def ps(self, shape, dtype, name=None):
        self._n += 1
        t = self.stack.enter_context(self.nc.psum_tensor("%s_%d" % (name or "ps", self._n), list(shape), dtype))
        return t

    def buf(self, name="b"):
        return self.P.buf(name)


class T:
    def __init__(self, t, b):
        self.t = t
        self.b = b


class HT:
    def __init__(self, t, bs):
        self.t = t
        self.bs = bs


def mk_h(ctx):
    t = ctx.sb([128, 8, 2048], F32, "h")
    return HT(t, [ctx.buf("h%d" % i) for i in range(8)])


def mk(ctx, shape, dtype, name, psum=False):
    t = ctx.ps(shape, dtype, name) if psum else ctx.sb(shape, dtype, name)
    b = ctx.buf(name)
    b.excl = psum
    return T(t, b)


def host_consts():
    ident = np.eye(128, dtype=np.float32)
    k = np.arange(128)[:, None]
    q = np.arange(128)[None, :]
    negmask = np.where(k > q, NEG, 0.0).astype(np.float32)
    tri = (k <= q).astype(np.float32)
    ones = np.ones((128, 128), np.float32)
    return {
        "c_ident_f": ident,
        "c_ident_b": ident.astype(ml_dtypes.bfloat16),
        "c_negmask_b": negmask.astype(ml_dtypes.bfloat16),
        "c_tri_f": tri,
        "c_ones_f": ones,
        "c_ones_b": ones.astype(ml_dtypes.bfloat16),
        "c_negmask4_b": np.tile(negmask, (1, 4)).astype(ml_dtypes.bfloat16),
    }


CONST_SPECS = {
    "c_ident_f": F32, "c_ident_b": BF16, "c_negmask_b": BF16, "c_tri_f": F32, "c_ones_f": F32, "c_ones_b": BF16,
    "c_negmask4_b": BF16,
}
CONST_COLS = {"c_negmask4_b": 512}


def load_consts(ctx, dram, names):
    out = {}
    for n in names:
        t = mk(ctx, [128, CONST_COLS.get(n, 128)], CONST_SPECS[n], n)
        ctx.P.dma("sp", lambda e, t=t, n=n: e.dma_start(out=t.t[:], in_=dram[n]), [], [t.b])
        out[n] = t
    return out


def declare_consts(nc, names):
    return {n: nc.dram_tensor(n, [128, CONST_COLS.get(n, 128)], CONST_SPECS[n], kind="ExternalInput").ap() for n in names}


def mm(P, out, lhsT, rhs, start, stop, reads, writes):
    return P.pe(lambda e: e.matmul(out, lhsT, rhs, start=start, stop=stop), reads, writes)


def tr(P, out, in_, ident, reads, writes):
    return P.pe(lambda e: e.transpose(out, in_, ident), reads, writes)


def actf(P, out, in_, func, reads, writes, bias=None, scale=None, accum_out=None):
    kw = {}
    if bias is not None:
        kw["bias"] = bias
    if scale is not None:
        kw["scale"] = scale
    if accum_out is not None:
        kw["accum_out"] = accum_out
    return P.act(lambda e: e.activation(out=out, in_=in_, func=func, **kw), reads, writes)


def cpy(P, eng, out, in_, reads, writes):
    if eng == "act":
        return P.act(lambda e: e.copy(out=out, in_=in_), reads, writes)
    if eng == "dve":
        return P.dve(lambda e: e.tensor_copy(out=out, in_=in_), reads, writes)
    return P.pool(lambda e: e.tensor_copy(out=out, in_=in_), reads, writes)


def tt(P, eng, out, in0, in1, op, reads, writes):
    f = lambda e: e.tensor_tensor(out=out, in0=in0, in1=in1, op=op)
    return (P.dve if eng == "dve" else P.pool)(f, reads, writes)


def ts(P, eng, out, in0, s1, s2, op0, op1, reads, writes, accum_out=None):
    if op1 is None:
        f = lambda e: e.tensor_scalar(out=out, in0=in0, scalar1=s1, scalar2=None, op0=op0)
    elif accum_out is not None:
        f = lambda e: e.tensor_scalar(out=out, in0=in0, scalar1=s1, scalar2=s2, op0=op0, op1=op1, accum_out=accum_out)
    else:
        f = lambda e: e.tensor_scalar(out=out, in0=in0, scalar1=s1, scalar2=s2, op0=op0, op1=op1)
    return (P.dve if eng == "dve" else P.pool)(f, reads, writes)


def stt(P, out, in0, scalar, in1, op0, op1, reads, writes):
    return P.dve(lambda e: e.scalar_tensor_tensor(out=out, in0=in0, scalar=scalar, in1=in1, op0=op0, op1=op1), reads, writes)


def dma(P, q, out, in_, reads, writes, slow=False):
    if slow:
        return P.dma(q, lambda e: e.dma_start(out=out, in_=in_, allow_slow_non_contiguous=True), reads, writes)
    return P.dma(q, lambda e: e.dma_start(out=out, in_=in_), reads, writes)


def memset(P, eng, ap, val, writes):
    f = lambda e: e.memset(ap, val)
    return (P.dve if eng == "dve" else P.pool)(f, [], writes)


def emit_transpose_h(ctx, h, hT_dst_view, ident_f, ps_tr):
    P = ctx.P
    hT = mk(ctx, [128, 16, TOK_OWN], BF16, "hT_sb")
    k = 0
    for g in range(2):
        for fc in range(16):
            pt = ps_tr[k % len(ps_tr)]
            for s in range(4):
                tt_ = g * 4 + s
                tr(P, pt.t[:, s * 128:(s + 1) * 128], h.t[:, tt_, fc * 128:(fc + 1) * 128], ident_f.t[:],
                   [h.bs[tt_], ident_f.b], [pt.b])
            cpy(P, "act" if k % 2 == 0 else "dve", hT.t[:, fc, g * 512:(g + 1) * 512], pt.t[:], [pt.b], [hT.b])
            k += 1
    ob = ctx.buf("hT_out")
    dma(P, "sp", hT_dst_view, hT.t[:], [hT.b], [ob])
    return ob


def emit_outproj_ln(ctx, h, yT_view, FC, w_out_view, lng_row, lnb_row, ps_acc, y_deps=()):
    P = ctx.P
    CW = 512 if FC <= 16 else 256
    NCC = 2048 // CW
    wt = [mk(ctx, [128, FC, CW], BF16, "wout%d" % i) for i in range(2)]
    yt = [mk(ctx, [128, FC, 128], BF16, "yTt%d" % i) for i in range(2)]
    g_bc = mk(ctx, [128, 2048], F32, "lng")
    b_bc = mk(ctx, [128, 2048], F32, "lnb")
    dma(P, "sp", g_bc.t[:], lng_row.partition_broadcast(128), [], [g_bc.b])
    dma(P, "sp", b_bc.t[:], lnb_row.partition_broadcast(128), [], [b_bc.b])
    k = 0
    for cc in range(NCC):
        w = wt[cc % 2]
        dma(P, "pool", w.t[:], w_out_view[:, :, cc * CW:(cc + 1) * CW], [], [w.b])
        for tt_ in range(8):
            y = yt[k % 2]
            dma(P, "sp", y.t[:], yT_view[:, :, tt_ * 128:(tt_ + 1) * 128], list(y_deps), [y.b])
            acc = ps_acc[k % len(ps_acc)]
            for fc in range(FC):
                mm(P, acc.t[:, 0:CW], y.t[:, fc, :], w.t[:, fc, :], fc == 0, fc == FC - 1, [y.b, w.b], [acc.b])
            hs = h.t[:, tt_, cc * CW:(cc + 1) * CW]
            stt(P, hs, hs, float(ALPHA), acc.t[:, 0:CW], ALU.mult, ALU.add, [acc.b, h.bs[tt_]], [h.bs[tt_]])
            k += 1
    epsc = mk(ctx, [128, 1], F32, "lneps")
    memset(P, "pool", epsc.t[:], float(LN_EPS), [epsc.b])
    for tt_ in range(8):
        stats = mk(ctx, [128, 4, 6], F32, "lnstats")
        mv = mk(ctx, [128, 2], F32, "lnmv")
        rstd = mk(ctx, [128, 1], F32, "lnrstd")
        rstd2 = mk(ctx, [128, 1], F32, "lnrstd2")
        hb = h.bs[tt_]
        for c in range(4):
            P.dve(lambda e, o=stats.t[:, c, :], i_=h.t[:, tt_, c * 512:(c + 1) * 512]: e.bn_stats(out=o, in_=i_), [hb], [stats.b])
        P.dve(lambda e, o=mv.t[:], i_=stats.t[:].rearrange("p a b -> p (a b)"): e.bn_aggr(out=o, in_=i_), [stats.b], [mv.b])
        actf(P, rstd.t[:], mv.t[:, 1:2], AF.Sqrt, [mv.b, epsc.b], [rstd.b], bias=epsc.t[:], scale=1.0)
        P.dve(lambda e, o=rstd2.t[:], i_=rstd.t[:]: e.reciprocal(out=o, in_=i_), [rstd.b], [rstd2.b])
        hs = h.t[:, tt_, :]
        ts(P, "dve", hs, hs, mv.t[:, 0:1], rstd2.t[:, 0:1], ALU.subtract, ALU.mult, [hb, mv.b, rstd2.b], [hb])
        tt(P, "pool", hs, hs, g_bc.t[:], ALU.mult, [hb, g_bc.b], [hb])
        tt(P, "dve", hs, hs, b_bc.t[:], ALU.add, [hb, b_bc.b], [hb])


import os


def emit_attention_head(ctx, *, kq_chunks, Vaug, gate, scale, bias_fn, consts, ps_s, ps_o, ps_t, pt_tiles, yT_sb, dv=128, nblk=32, gate_c0=0):
    P = ctx.P
    ident_b = consts["c_ident_b"]
    negmask = consts["c_negmask_b"]
    ytok = [mk(ctx, [128, dv], BF16, "ytok%d" % i) for i in range(2)]
    rl = [mk(ctx, [128, 1], F32, "rl%d" % i) for i in range(2)]
    nch = len(kq_chunks)
    sidx = 0
    for i in range(nblk):
        bias = bias_fn(i) if bias_fn is not None else None
        po = ps_o[i % len(ps_o)]
        groups = [list(range(g, min(g + 4, i + 1))) for g in range(0, i + 1, 4)]

        def emit_s(js, st):
            for jj, j in enumerate(js):
                dst = st.t[:, jj * 128:(jj + 1) * 128]
                for ci, (KT, QT, npart) in enumerate(kq_chunks):
                    mm(P, dst, KT.t[0:npart, j * 128:(j + 1) * 128], QT.t[0:npart, i * 128:(i + 1) * 128],
                       ci == 0, (j != i) and ci == nch - 1, [KT.b, QT.b], [st.b])
                if j == i:
                    mm(P, dst, ident_b.t[:], negmask.t[:], False, True, [ident_b.b, negmask.b], [st.b])

        def emit_exp_pv(js, st, pt):
            if bias is None:
                n = len(js) * 128
                actf(P, pt.t[:, 0:n], st.t[:, 0:n], AF.Exp, [st.b], [pt.b], scale=float(scale))
            else:
                for jj, j in enumerate(js):
                    actf(P, pt.t[:, jj * 128:(jj + 1) * 128], st.t[:, jj * 128:(jj + 1) * 128], AF.Exp,
                         [st.b, bias.b], [pt.b], bias=bias.t[:, j:j + 1], scale=float(scale))
            for jj, j in enumerate(js):
                mm(P, po.t[:, 0:dv + 1], pt.t[:, jj * 128:(jj + 1) * 128], Vaug.t[:, j, :], j == 0, j == i,
                   [pt.b, Vaug.b], [po.b])

        sts = []
        for gi, js in enumerate(groups):
            st = ps_s[sidx % len(ps_s)]
            pt = pt_tiles[sidx % len(pt_tiles)]
            sidx += 1
            emit_s(js, st)
            sts.append((js, st, pt))
            if gi >= 1:
                emit_exp_pv(*sts[gi - 1])
        emit_exp_pv(*sts[-1])
        r = rl[i % 2]
        yt = ytok[i % 2]
        P.dve(lambda e, o=r.t[:], i_=po.t[:, dv:dv + 1]: e.reciprocal(out=o, in_=i_), [po.b], [r.b])
        stt(P, yt.t[:], po.t[:, 0:dv], r.t[:, 0:1], gate.t[:, i, gate_c0:gate_c0 + dv], ALU.mult, ALU.mult, [po.b, r.b, gate.b], [yt.b])
        tr(P, ps_t.t[0:dv, 0:128], yt.t[:, 0:dv], ident_b.t[:], [yt.b, ident_b.b], [ps_t.b])
        cpy(P, "act", yT_sb.t[0:dv, i * 128:(i + 1) * 128], ps_t.t[0:dv, 0:128], [ps_t.b], [yT_sb.b])


def stage_fox(ctx, *, hT_tile_view, w_in, f_bias_row, yT_out, r, consts, nheads=4, nblk=32, W=2048):
    P = ctx.P
    wv = w_in.rearrange("(kc p) n -> p kc n", p=128)
    tri = consts["c_tri_f"]
    ones_f = consts["c_ones_f"]
    ps_a = [mk(ctx, [128, 512], F32, "psA%d" % i, psum=True) for i in range(2)]
    ps_s = [mk(ctx, [128, 512], F32, "psS%d" % i, psum=True) for i in range(2)]
    ps_o = [mk(ctx, [128, 512], F32, "psO%d" % i, psum=True) for i in range(2)]
    ps_vz = mk(ctx, [128, 512], F32, "psVZ", psum=True)
    ps_t = mk(ctx, [128, 1024], BF16, "psT", psum=True)
    Wt = [mk(ctx, [128, 16, 512], BF16, "W%d" % i) for i in range(2)]
    Wf = mk(ctx, [128, 16, 4], BF16, "Wf")
    hTt = [mk(ctx, [128, 16, 512], BF16, "hTt%d" % i) for i in range(2)]
    QT = mk(ctx, [128, SEQ], BF16, "QT")
    KT = mk(ctx, [128, SEQ], BF16, "KT")
    Vaug = mk(ctx, [128, 32, 129], BF16, "Vaug")
    SZ = mk(ctx, [128, 32, 128], F32, "SZ")
    Fraw = mk(ctx, [128, 32, 4], F32, "Fraw")
    Fe = mk(ctx, [128, 32, 4], F32, "Fe")
    Fl = mk(ctx, [128, 32, 4], F32, "Fl")
    Ltab = mk(ctx, [128, 32, 4], F32, "Ltab")
    Tot = mk(ctx, [128, 32, 4], F32, "Tot")
    Incl = mk(ctx, [128, 32, 4], F32, "Incl")
    fb = mk(ctx, [128, 4], F32, "fb")
    onesrow = mk(ctx, [128, 32], F32, "onesrow")
    biast = [mk(ctx, [128, 32], F32, "biast%d" % i) for i in range(2)]
    pt_tiles = [mk(ctx, [128, 512], BF16, "PT%d" % i) for i in range(3)]
    yT_sb = [mk(ctx, [128, SEQ], BF16, "yTsb%d" % i) for i in range(2)]

    memset(P, "pool", Vaug.t[:, :, 128:129], 1.0, [Vaug.b])
    memset(P, "pool", onesrow.t[:], 1.0, [onesrow.b])
    dma(P, "pool", Wf.t[:], wv[:, :, 4 * W + 4 * r:4 * W + 4 * r + 4], [], [Wf.b])
    dma(P, "sp", fb.t[:], f_bias_row[:, 4 * r:4 * r + 4].partition_broadcast(128), [], [fb.b])

    hidx = 0
    for hl in range(nheads):
        hg = 4 * r + hl
        Wh = Wt[hl % 2]
        for part in range(4):
            c0 = part * W + hg * 128
            dma(P, "pool", Wh.t[:, :, part * 128:(part + 1) * 128], wv[:, :, c0:c0 + 128], [], [Wh.b])
        for n in range(8):
            ht = hTt[hidx % 2]
            hidx += 1
            dma(P, "sp", ht.t[:], hT_tile_view(n), [], [ht.b])
            for part, dstT, evac in ((0, QT, "act"), (1, KT, "dve")):
                acc = ps_a[part]
                for kc in range(16):
                    mm(P, acc.t[:], Wh.t[:, kc, part * 128:(part + 1) * 128], ht.t[:, kc, :], kc == 0, kc == 15,
                       [Wh.b, ht.b], [acc.b])
                cpy(P, evac, dstT.t[:, n * 512:(n + 1) * 512], acc.t[:], [acc.b], [dstT.b])
            for s in range(4):
                blk = n * 4 + s
                acc = ps_vz
                for kc in range(16):
                    mm(P, acc.t[:, 0:256], ht.t[:, kc, s * 128:(s + 1) * 128], Wh.t[:, kc, 256:512], kc == 0, kc == 15,
                       [Wh.b, ht.b], [acc.b])
                if hl == 0:
                    for kc in range(16):
                        mm(P, acc.t[:, 256:260], ht.t[:, kc, s * 128:(s + 1) * 128], Wf.t[:, kc, :], kc == 0, kc == 15,
                           [Wf.b, ht.b], [acc.b])
                cpy(P, "dve", Vaug.t[:, blk, 0:128], acc.t[:, 0:128], [acc.b], [Vaug.b])
                actf(P, SZ.t[:, blk, :], acc.t[:, 128:256], AF.Silu, [acc.b], [SZ.b])
                if hl == 0:
                    cpy(P, "dve", Fraw.t[:, blk, :], acc.t[:, 256:260], [acc.b], [Fraw.b])

        if hl == 0:
            tt(P, "dve", Fe.t[:], Fraw.t[:], fb.t[:].unsqueeze(1).to_broadcast([128, 32, 4]), ALU.add, [Fraw.b, fb.b], [Fe.b])
            actf(P, Fl.t[:], Fe.t[:], AF.Exp, [Fe.b], [Fl.b], scale=-1.0)
            actf(P, Fe.t[:], Fl.t[:], AF.Ln, [Fl.b], [Fe.b], bias=1.0, scale=1.0)
            accL, accT = ps_a[0], ps_a[1]
            fr2 = Fe.t[:].rearrange("p a b -> p (a b)")
            mm(P, accL.t[:, 0:128], tri.t[:], fr2, True, True, [tri.b, Fe.b], [accL.b])
            mm(P, accT.t[:, 0:128], ones_f.t[:], fr2, True, True, [ones_f.b, Fe.b], [accT.b])
            cpy(P, "act", Ltab.t[:].rearrange("p a b -> p (a b)"), accL.t[:, 0:128], [accL.b], [Ltab.b])
            cpy(P, "act", Tot.t[:].rearrange("p a b -> p (a b)"), accT.t[:, 0:128], [accT.b], [Tot.b])
            for hh in range(4):
                P.dve(lambda e, o=Incl.t[:, :, hh], d0=onesrow.t[:], d1=Tot.t[:, :, hh]: e.tensor_tensor_scan(
                    out=o, data0=d0, data1=d1, initial=0.0, op0=ALU.mult, op1=ALU.add), [Tot.b, onesrow.b], [Incl.b])
            tt(P, "dve", Ltab.t[:, 1:32, :], Ltab.t[:, 1:32, :], Incl.t[:, 0:31, :], ALU.add, [Ltab.b, Incl.b], [Ltab.b])

        def bias_fn(i, hl=hl):
            bt = biast[i % 2]
            ts(P, "dve", bt.t[:, 0:i + 1], Ltab.t[:, 0:i + 1, hl], Incl.t[:, i, hl:hl + 1], None, ALU.subtract, None,
               [Ltab.b, Incl.b], [bt.b])
            return bt

        ysb = yT_sb[hl % 2]
        emit_attention_head(ctx, kq_chunks=[(KT, QT, 128)], Vaug=Vaug, gate=SZ, scale=128 ** -0.5, bias_fn=bias_fn,
                            consts=consts, ps_s=ps_s, ps_o=ps_o, ps_t=ps_t, pt_tiles=pt_tiles, yT_sb=ysb, nblk=nblk)
        ob = ctx.buf("yTout")
        dma(P, "sp", yT_out[hl * 128:(hl + 1) * 128, :], ysb.t[:], [ysb.b], [ob])
        ctx.out_bufs.append(ob)


TWO_PI_S = 6.2831845


def emit_rope_tables(ctx, pos_row, rope_c, ntok, name):
    P = ctx.P
    CH = 512
    posi = mk(ctx, [64, CH], I32, name + "posi")
    u = mk(ctx, [64, CH], F32, name + "u")
    v = mk(ctx, [64, CH], F32, name + "v")
    ki = mk(ctx, [64, CH], I32, name + "ki")
    kf = mk(ctx, [64, CH], F32, name + "kf")
    m = mk(ctx, [64, CH], F32, name + "m")
    cos = mk(ctx, [64, ntok], F32, name + "cos")
    sin = mk(ctx, [64, ntok], F32, name + "sin")
    for ch in range(ntok // CH):
        sl = slice(ch * CH, (ch + 1) * CH)
        dma(P, "sp", posi.t[:], pos_row[:, sl].partition_broadcast(64), [], [posi.b])
        cpy(P, "dve", u.t[:], posi.t[:], [posi.b], [u.b])
        ts(P, "dve", u.t[:], u.t[:], rope_c.t[0:64, 0:1], None, ALU.mult, None, [u.b, rope_c.b], [u.b])
        for off, dst in ((0.0, sin), (0.25, cos)):
            ts(P, "dve", v.t[:], u.t[:], float(off), None, ALU.add, None, [u.b], [v.b])
            cpy(P, "dve", ki.t[:], v.t[:], [v.b], [ki.b])
            cpy(P, "dve", kf.t[:], ki.t[:], [ki.b], [kf.b])
            tt(P, "dve", v.t[:], v.t[:], kf.t[:], ALU.subtract, [v.b, kf.b], [v.b])
            ts(P, "dve", m.t[:], v.t[:], 0.5, None, ALU.is_gt, None, [v.b], [m.b])
            tt(P, "dve", v.t[:], v.t[:], m.t[:], ALU.subtract, [v.b, m.b], [v.b])
            ts(P, "dve", m.t[:], v.t[:], -0.5, None, ALU.is_lt, None, [v.b], [m.b])
            tt(P, "dve", v.t[:], v.t[:], m.t[:], ALU.add, [v.b, m.b], [v.b])
            actf(P, dst.t[:, sl], v.t[:], AF.Sin, [v.b], [dst.b], scale=TWO_PI_S)
        ts(P, "dve", sin.t[:, sl], sin.t[:, sl], rope_c.t[0:64, 1:2], None, ALU.mult, None, [sin.b, rope_c.b], [sin.b])
    return cos, sin


def stage_mla_a(ctx, *, hT_tile_view, ntiles, w_in, pos_row, latT_out_view, consts, rope_c):
    P = ctx.P
    wv = w_in.rearrange("(kc p) n -> p kc n", p=128)
    ones_f = consts["c_ones_f"]
    ntok = ntiles * 512
    cos, sin = emit_rope_tables(ctx, pos_row, rope_c, ntok, "ra")
    Wl = mk(ctx, [128, 16, 1088], BF16, "Wl")
    Wsw = mk(ctx, [128, 16, 64], BF16, "Wsw")
    for c4 in range(4):
        dma(P, "pool", Wl.t[:, :, c4 * 272:(c4 + 1) * 272], wv[:, :, c4 * 272:(c4 + 1) * 272], [], [Wl.b])
    dma(P, "pool", Wsw.t[:, :, 0:32], wv[:, :, 1056:1088], [], [Wsw.b])
    dma(P, "pool", Wsw.t[:, :, 32:64], wv[:, :, 1024:1056], [], [Wsw.b])
    hTt = [mk(ctx, [128, 16, 512], BF16, "hTa%d" % i) for i in range(2)]
    ps = [mk(ctx, [128, 512], F32, "psa%d" % i, psum=True) for i in range(4)]
    ps_ss = mk(ctx, [128, 512], F32, "psss", psum=True)
    lat_f = mk(ctx, [128, 4, 512], F32, "lat_f")
    sq = mk(ctx, [128, 4, 512], F32, "sq")
    lnv = mk(ctx, [128, 512], F32, "lnv")
    rstd = mk(ctx, [128, 512], F32, "rstdbc")
    epsc = mk(ctx, [128, 1], F32, "rmseps")
    memset(P, "pool", epsc.t[:], float(RMS_EPS), [epsc.b])
    XT = [mk(ctx, [128, 9, 512], BF16, "XT%d" % i) for i in range(2)]
    t1 = mk(ctx, [64, 512], F32, "t1")
    t2 = mk(ctx, [64, 512], F32, "t2")
    for i in range(2):
        memset(P, "pool", XT[i].t[:, 8, :], 0.0, [XT[i].b])
    k = 0
    for n in range(ntiles):
        ht = hTt[n % 2]
        xt = XT[n % 2]
        dma(P, "sp", ht.t[:], hT_tile_view(n), [], [ht.b])
        for lat in range(2):
            for c in range(4):
                acc = ps[k % 4]
                k += 1
                c0 = lat * 512 + c * 128
                for kc in range(16):
                    mm(P, acc.t[:], Wl.t[:, kc, c0:c0 + 128], ht.t[:, kc, :], kc == 0, kc == 15, [Wl.b, ht.b], [acc.b])
                cpy(P, "act", lat_f.t[:, c, :], acc.t[:], [acc.b], [lat_f.b])
                actf(P, sq.t[:, c, :], acc.t[:], AF.Square, [acc.b], [sq.b])
            for c in range(4):
                mm(P, ps_ss.t[:], ones_f.t[:], sq.t[:, c, :], c == 0, c == 3, [ones_f.b, sq.b], [ps_ss.b])
            actf(P, lnv.t[:], ps_ss.t[:], AF.Ln, [ps_ss.b, epsc.b], [lnv.b], bias=epsc.t[:], scale=1.0 / 512.0)
            actf(P, rstd.t[:], lnv.t[:], AF.Exp, [lnv.b], [rstd.b], scale=-0.5)
            for c in range(4):
                tt(P, "dve" if c % 2 == 0 else "pool", xt.t[:, lat * 4 + c, :], lat_f.t[:, c, :], rstd.t[:], ALU.mult,
                   [lat_f.b, rstd.b], [xt.b])
        a1 = ps[k % 4]
        a2 = ps[(k + 1) % 4]
        k += 2
        for kc in range(16):
            mm(P, a1.t[0:64, :], Wl.t[:, kc, 1024:1088], ht.t[:, kc, :], kc == 0, kc == 15, [Wl.b, ht.b], [a1.b])
        for kc in range(16):
            mm(P, a2.t[0:64, :], Wsw.t[:, kc, :], ht.t[:, kc, :], kc == 0, kc == 15, [Wsw.b, ht.b], [a2.b])
        tt(P, "dve", t1.t[:], a1.t[0:64, :], cos.t[:, n * 512:(n + 1) * 512], ALU.mult, [a1.b, cos.b], [t1.b])
        tt(P, "dve", t2.t[:], a2.t[0:64, :], sin.t[:, n * 512:(n + 1) * 512], ALU.mult, [a2.b, sin.b], [t2.b])
        tt(P, "dve", xt.t[0:64, 8, :], t1.t[:], t2.t[:], ALU.add, [t1.b, t2.b], [xt.b])
        ob = ctx.buf("latout")
        dma(P, "sp", latT_out_view(n), xt.t[:], [xt.b], [ob])
        ctx.out_bufs.append(ob)


def stage_mla_b(ctx, *, hT_tile_view, latT_tile_view, kpe_view, w_z, q_norm_v, kv_norm_v, w_q_up, w_kv_up, pos_row,
                yT_out, r, consts, rope_c, nheads=4):
    P = ctx.P
    wv = w_z.rearrange("(kc p) n -> p kc n", p=128)
    wq = w_q_up.rearrange("(c p) n -> p c n", p=128)
    wkv = w_kv_up.rearrange("(c p) n -> p c n", p=128)
    cos, sin = emit_rope_tables(ctx, pos_row, rope_c, SEQ, "rb")
    ps_a = [mk(ctx, [128, 512], F32, "psA%d" % i, psum=True) for i in range(2)]
    ps_s = [mk(ctx, [128, 512], F32, "psS%d" % i, psum=True) for i in range(2)]
    ps_o = [mk(ctx, [128, 512], F32, "psO%d" % i, psum=True) for i in range(2)]
    ps_vz = mk(ctx, [128, 512], F32, "psVZ", psum=True)
    ps_t = mk(ctx, [128, 1024], BF16, "psT", psum=True)
    KPE = mk(ctx, [64, SEQ], BF16, "KPE")
    dma(P, "sp", KPE.t[:], kpe_view, [], [KPE.b])
    qn = mk(ctx, [128, 4], F32, "qn")
    kvn = mk(ctx, [128, 4], F32, "kvn")
    dma(P, "sp", qn.t[:], q_norm_v, [], [qn.b], slow=True)
    dma(P, "sp", kvn.t[:], kv_norm_v, [], [kvn.b], slow=True)
    Wq_f = mk(ctx, [128, 4, 256], F32, "Wq_f")
    Wkv_f = mk(ctx, [128, 4, 256], F32, "Wkv_f")
    Wq = [mk(ctx, [128, 4, 256], BF16, "Wq%d" % i) for i in range(2)]
    Wkv = [mk(ctx, [128, 4, 256], BF16, "Wkv%d" % i) for i in range(2)]
    Wz = [mk(ctx, [128, 16, 128], BF16, "Wz%d" % i) for i in range(2)]
    hTt = [mk(ctx, [128, 16, 512], BF16, "hTb%d" % i) for i in range(2)]
    XTt = [mk(ctx, [128, 8, 512], BF16, "XTb%d" % i) for i in range(2)]
    QT = mk(ctx, [128, SEQ], BF16, "QT")
    KT = mk(ctx, [128, SEQ], BF16, "KT")
    QR = mk(ctx, [64, SEQ], BF16, "QR")
    Vaug = mk(ctx, [128, 32, 129], BF16, "Vaug")
    SZ = mk(ctx, [128, 32, 128], BF16, "SZ")
    t1 = mk(ctx, [64, 512], F32, "t1")
    t2 = mk(ctx, [64, 512], F32, "t2")
    pt_tiles = [mk(ctx, [128, 512], BF16, "PT%d" % i) for i in range(3)]
    yT_sb = [mk(ctx, [128, SEQ], BF16, "yTsb%d" % i) for i in range(2)]
    memset(P, "pool", Vaug.t[:, :, 128:129], 1.0, [Vaug.b])
    tix = 0
    for hl in range(nheads):
        hg = 4 * r + hl
        q0 = hg * 192
        k0 = hg * 256
        dma(P, "sp", Wq_f.t[:, :, 0:192], wq[:, :, q0:q0 + 192], [], [Wq_f.b])
        dma(P, "sp", Wq_f.t[:, :, 192:224], wq[:, :, q0 + 160:q0 + 192], [], [Wq_f.b])
        dma(P, "sp", Wq_f.t[:, :, 224:256], wq[:, :, q0 + 128:q0 + 160], [], [Wq_f.b])
        dma(P, "sp", Wkv_f.t[:], wkv[:, :, k0:k0 + 256], [], [Wkv_f.b])
        wqh, wkvh, wzh = Wq[hl % 2], Wkv[hl % 2], Wz[hl % 2]
        for c in range(4):
            ts(P, "dve", wqh.t[:, c, :], Wq_f.t[:, c, :], qn.t[:, c:c + 1], None, ALU.mult, None, [Wq_f.b, qn.b], [wqh.b])
            ts(P, "dve", wkvh.t[:, c, :], Wkv_f.t[:, c, :], kvn.t[:, c:c + 1], None, ALU.mult, None, [Wkv_f.b, kvn.b], [wkvh.b])
        z0 = hg * 128
        dma(P, "pool", wzh.t[:], wv[:, :, z0:z0 + 128], [], [wzh.b])
        for n in range(8):
            ht = hTt[tix % 2]
            xt = XTt[tix % 2]
            tix += 1
            dma(P, "sp", ht.t[:], hT_tile_view(n), [], [ht.b])
            dma(P, "sp", xt.t[:], latT_tile_view(n), [], [xt.b])
            tsl = slice(n * 512, (n + 1) * 512)
            for c in range(4):
                mm(P, ps_a[0].t[:], wqh.t[:, c, 0:128], xt.t[:, c, :], c == 0, c == 3, [wqh.b, xt.b], [ps_a[0].b])
            cpy(P, "act", QT.t[:, tsl], ps_a[0].t[:], [ps_a[0].b], [QT.b])
            for c in range(4):
                mm(P, ps_a[1].t[:], wkvh.t[:, c, 0:128], xt.t[:, 4 + c, :], c == 0, c == 3, [wkvh.b, xt.b], [ps_a[1].b])
            cpy(P, "dve", KT.t[:, tsl], ps_a[1].t[:], [ps_a[1].b], [KT.b])
            for c in range(4):
                mm(P, ps_a[0].t[0:64, :], wqh.t[:, c, 128:192], xt.t[:, c, :], c == 0, c == 3, [wqh.b, xt.b], [ps_a[0].b])
            for c in range(4):
                mm(P, ps_a[1].t[0:64, :], wqh.t[:, c, 192:256], xt.t[:, c, :], c == 0, c == 3, [wqh.b, xt.b], [ps_a[1].b])
            tt(P, "dve", t1.t[:], ps_a[0].t[0:64, :], cos.t[:, tsl], ALU.mult, [ps_a[0].b, cos.b], [t1.b])
            tt(P, "dve", t2.t[:], ps_a[1].t[0:64, :], sin.t[:, tsl], ALU.mult, [ps_a[1].b, sin.b], [t2.b])
            tt(P, "pool", QR.t[:, tsl], t1.t[:], t2.t[:], ALU.add, [t1.b, t2.b], [QR.b])
            for s in range(4):
                blk = n * 4 + s
                for c in range(4):
                    mm(P, ps_vz.t[:, 0:128], xt.t[:, 4 + c, s * 128:(s + 1) * 128], wkvh.t[:, c, 128:256], c == 0, c == 3,
                       [wkvh.b, xt.b], [ps_vz.b])
                for kc in range(16):
                    mm(P, ps_vz.t[:, 128:256], ht.t[:, kc, s * 128:(s + 1) * 128], wzh.t[:, kc, :], kc == 0, kc == 15,
                       [wzh.b, ht.b], [ps_vz.b])
                cpy(P, "dve", Vaug.t[:, blk, 0:128], ps_vz.t[:, 0:128], [ps_vz.b], [Vaug.b])
                actf(P, SZ.t[:, blk, :], ps_vz.t[:, 128:256], AF.Silu, [ps_vz.b], [SZ.b])
        ysb = yT_sb[hl % 2]
        emit_attention_head(ctx, kq_chunks=[(KT, QT, 128), (KPE, QR, 64)], Vaug=Vaug, gate=SZ, scale=192 ** -0.5,
                            bias_fn=None, consts=consts, ps_s=ps_s, ps_o=ps_o, ps_t=ps_t, pt_tiles=pt_tiles, yT_sb=ysb)
        ob = ctx.buf("yTout")
        dma(P, "sp", yT_out[hl * 128:(hl + 1) * 128, :], ysb.t[:], [ysb.b], [ob])
        ctx.out_bufs.append(ob)


def stage_ssd(ctx, *, hT_tile_view, w_in, conv_w, conv_b, dt_bias, a_log, d_skip, norm_w, yT_out, r, consts, ngroups=2, ntiles=8, offs=None):
    P = ctx.P
    if offs is None:
        offs = dict(z0=0, x0=4096, B0=8192, C0=9216, dt0=10240, cx0=0, cB0=4096, cC0=5120)
    wv = w_in.rearrange("(kc p) n -> p kc n", p=128)
    tri = consts["c_tri_f"]
    ones_f = consts["c_ones_f"]
    ident_f = consts["c_ident_f"]
    ident_b = consts["c_ident_b"]
    negmask4 = consts["c_negmask4_b"]
    pA = mk(ctx, [128, 512], F32, "pA", psum=True)
    pB = mk(ctx, [128, 512], F32, "pB", psum=True)
    pD = [mk(ctx, [128, 512], F32, "pD%d" % i, psum=True) for i in range(2)]
    pY = mk(ctx, [128, 512], F32, "pY", psum=True)
    pO = mk(ctx, [128, 512], F32, "pO", psum=True)
    pS = mk(ctx, [128, 512], F32, "pS", psum=True)
    pM = mk(ctx, [128, 512], F32, "pM", psum=True)
    pAB = [pA, pB]

    Wg = mk(ctx, [128, 16, 1288], BF16, "Wg")
    hTt = [mk(ctx, [128, 16, 512], BF16, "hTs%d" % i) for i in range(2)]
    pre = mk(ctx, [128, 6, 515], F32, "pre")
    cacc = [mk(ctx, [128, 512], F32, "cacc%d" % i) for i in range(2)]
    xTf = mk(ctx, [128, 4, 512], F32, "xTf")
    BTf = mk(ctx, [128, 512], F32, "BTf")
    BT = mk(ctx, [128, 512], BF16, "BT")
    CT = mk(ctx, [128, 512], BF16, "CT")
    Xtok = mk(ctx, [128, 4, 512], BF16, "Xtok")
    Btok = mk(ctx, [128, 4, 128], BF16, "Btok")
    SZ = mk(ctx, [128, 4, 512], F32, "SZ")
    dtt = mk(ctx, [128, 4, 8], F32, "dtt")
    dte_ = mk(ctx, [128, 4, 8], F32, "dte_")
    dAt = mk(ctx, [128, 4, 8], F32, "dAt")
    cw = mk(ctx, [128, 6, 4], F32, "cw")
    cb = mk(ctx, [128, 6], F32, "cb")
    dtb = mk(ctx, [128, 8], F32, "dtb")
    A_bc = mk(ctx, [128, 8], F32, "A_bc")
    D_bc = mk(ctx, [128, 8], F32, "D_bc")
    nw_bc = mk(ctx, [128, 512], F32, "nw_bc")
    epsc = mk(ctx, [128, 1], F32, "eps")
    acsS = mk(ctx, [128, 16], F32, "acsS")
    nacs = mk(ctx, [128, 8], F32, "nacs")
    Eacs = mk(ctx, [128, 8], F32, "Eacs")
    cd = mk(ctx, [128, 8], F32, "cd")
    dte = mk(ctx, [128, 8], F32, "dte")
    xdt = mk(ctx, [128, 8, 64], BF16, "xdt")
    xdt2 = mk(ctx, [128, 8, 64], BF16, "xdt2")
    R = mk(ctx, [128, 8, 128], F32, "R")
    G_sb = mk(ctx, [128, 128], BF16, "G_sb")
    Dk = mk(ctx, [128, 8, 128], BF16, "Dk")
    M = mk(ctx, [128, 8, 128], BF16, "M")
    yo = mk(ctx, [128, 8, 64], F32, "yo")
    y1 = mk(ctx, [128, 512], F32, "y1")
    xd = mk(ctx, [128, 8, 64], F32, "xd")
    y3 = mk(ctx, [128, 512], F32, "y3")
    junk = mk(ctx, [128, 512], F32, "junk")
    ssq = mk(ctx, [128, 1], F32, "ssq")
    lnv = mk(ctx, [128, 1], F32, "lnv1")
    rstd = mk(ctx, [128, 1], F32, "rstd1")
    y4 = mk(ctx, [128, 512], F32, "y4")
    prev = mk(ctx, [128, 8, 64], F32, "prev")
    prevt = mk(ctx, [128, 8, 64], F32, "prevt")
    prev_bf = mk(ctx, [128, 512], BF16, "prev_bf")
    yT_sb = [mk(ctx, [128, 4, 512], BF16, "yTs%d" % i) for i in range(2)]
    memset(P, "pool", epsc.t[:], float(RMS_EPS), [epsc.b])
    yv = yT_out.rearrange("(c p) t -> p c t", p=128)
    hidx = 0
    for g in range(ngroups):
        G = 2 * r + g
        o = offs
        dma(P, "pool", Wg.t[:, :, 0:512], wv[:, :, o["x0"] + G * 512:o["x0"] + (G + 1) * 512], [], [Wg.b])
        dma(P, "pool", Wg.t[:, :, 512:640], wv[:, :, o["B0"] + G * 128:o["B0"] + (G + 1) * 128], [], [Wg.b])
        dma(P, "pool", Wg.t[:, :, 640:768], wv[:, :, o["C0"] + G * 128:o["C0"] + (G + 1) * 128], [], [Wg.b])
        dma(P, "pool", Wg.t[:, :, 768:1280], wv[:, :, o["z0"] + G * 512:o["z0"] + (G + 1) * 512], [], [Wg.b])
        dma(P, "pool", Wg.t[:, :, 1280:1288], wv[:, :, o["dt0"] + G * 8:o["dt0"] + (G + 1) * 8], [], [Wg.b])
        ch0 = [o["cx0"] + G * 512 + c * 128 for c in range(4)] + [o["cB0"] + G * 128, o["cC0"] + G * 128]
        for c in range(6):
            dma(P, "sp", cw.t[:, c, :], conv_w[:, ch0[c]:ch0[c] + 128].rearrange("k p -> p k"), [], [cw.b], slow=True)
            dma(P, "sp", cb.t[:, c:c + 1], conv_b[:, ch0[c]:ch0[c] + 128].rearrange("o p -> p o"), [], [cb.b], slow=True)
        dma(P, "sp", dtb.t[:], dt_bias[:, G * 8:(G + 1) * 8].partition_broadcast(128), [], [dtb.b])
        dma(P, "sp", A_bc.t[:], a_log[:, G * 8:(G + 1) * 8].partition_broadcast(128), [], [A_bc.b])
        dma(P, "sp", D_bc.t[:], d_skip[:, G * 8:(G + 1) * 8].partition_broadcast(128), [], [D_bc.b])
        dma(P, "sp", nw_bc.t[:], norm_w[:, G * 512:(G + 1) * 512].partition_broadcast(128), [], [nw_bc.b])
        actf(P, A_bc.t[:], A_bc.t[:], AF.Exp, [A_bc.b], [A_bc.b])
        ts(P, "dve", A_bc.t[:], A_bc.t[:], -1.0, None, ALU.mult, None, [A_bc.b], [A_bc.b])
        memset(P, "pool", pre.t[:, :, 0:3], 0.0, [pre.b])
        memset(P, "pool", prev.t[:], 0.0, [prev.b])
        memset(P, "pool", prev_bf.t[:], 0.0, [prev_bf.b])
        k = 0
        for n in range(ntiles):
            ht = hTt[hidx % 2]
            hidx += 1
            dma(P, "sp", ht.t[:], hT_tile_view(n), [], [ht.b])
            if n > 0:
                cpy(P, "pool", pre.t[:, :, 0:3], pre.t[:, :, 512:515], [pre.b], [pre.b])
            for c in range(6):
                acc = pAB[k % 2]
                k += 1
                for kc in range(16):
                    mm(P, acc.t[:], Wg.t[:, kc, c * 128:(c + 1) * 128], ht.t[:, kc, :], kc == 0, kc == 15, [Wg.b, ht.b], [acc.b])
                cpy(P, "act", pre.t[:, c, 3:515], acc.t[:], [acc.b], [pre.b])
            for c in range(6):
                ca = cacc[c % 2]
                eng = "dve" if c % 2 == 0 else "pool"
                ts(P, eng, ca.t[:], pre.t[:, c, 0:512], cw.t[:, c, 0:1], None, ALU.mult, None, [pre.b, cw.b], [ca.b])
                for kk in range(1, 4):
                    if eng == "dve":
                        stt(P, ca.t[:], pre.t[:, c, kk:kk + 512], cw.t[:, c, kk:kk + 1], ca.t[:], ALU.mult, ALU.add,
                            [pre.b, cw.b, ca.b], [ca.b])
                    else:
                        tmp = junk
                        ts(P, "pool", tmp.t[:], pre.t[:, c, kk:kk + 512], cw.t[:, c, kk:kk + 1], None, ALU.mult, None,
                           [pre.b, cw.b], [tmp.b])
                        tt(P, "pool", ca.t[:], ca.t[:], tmp.t[:], ALU.add, [ca.b, tmp.b], [ca.b])
                if c < 4:
                    actf(P, xTf.t[:, c, :], ca.t[:], AF.Silu, [ca.b, cb.b], [xTf.b], bias=cb.t[:, c:c + 1], scale=1.0)
                elif c == 4:
                    actf(P, BTf.t[:], ca.t[:], AF.Silu, [ca.b, cb.b], [BTf.b], bias=cb.t[:, c:c + 1], scale=1.0)
                    cpy(P, "pool", BT.t[:], BTf.t[:], [BTf.b], [BT.b])
                else:
                    actf(P, CT.t[:], ca.t[:], AF.Silu, [ca.b, cb.b], [CT.b], bias=cb.t[:, c:c + 1], scale=1.0)
            for s in range(4):
                acc = pAB[k % 2]
                k += 1
                for c in range(4):
                    tr(P, acc.t[:, c * 128:(c + 1) * 128], xTf.t[:, c, s * 128:(s + 1) * 128], ident_f.t[:], [xTf.b, ident_f.b], [acc.b])
                cpy(P, "dve", Xtok.t[:, s, :], acc.t[:], [acc.b], [Xtok.b])
                acc = pAB[k % 2]
                k += 1
                tr(P, acc.t[:, 0:128], BTf.t[:, s * 128:(s + 1) * 128], ident_f.t[:], [BTf.b, ident_f.b], [acc.b])
                for kc in range(16):
                    mm(P, acc.t[:, 128:136], ht.t[:, kc, s * 128:(s + 1) * 128], Wg.t[:, kc, 1280:1288], kc == 0, kc == 15,
                       [Wg.b, ht.b], [acc.b])
                cpy(P, "act", Btok.t[:, s, :], acc.t[:, 0:128], [acc.b], [Btok.b])
                tt(P, "dve", dtt.t[:, s, :], acc.t[:, 128:136], dtb.t[:], ALU.add, [acc.b, dtb.b], [dtt.b])
                acc = pAB[k % 2]
                k += 1
                for kc in range(16):
                    mm(P, acc.t[:], ht.t[:, kc, s * 128:(s + 1) * 128], Wg.t[:, kc, 768:1280], kc == 0, kc == 15, [Wg.b, ht.b], [acc.b])
                actf(P, SZ.t[:, s, :], acc.t[:], AF.Silu, [acc.b], [SZ.b])
            actf(P, dte_.t[:], dtt.t[:], AF.Exp, [dtt.b], [dte_.b])
            actf(P, dtt.t[:], dte_.t[:], AF.Ln, [dte_.b], [dtt.b], bias=1.0, scale=1.0)
            tt(P, "dve", dAt.t[:], dtt.t[:], A_bc.t[:].unsqueeze(1).to_broadcast([128, 4, 8]), ALU.mult, [dtt.b, A_bc.b], [dAt.b])
            ysb = yT_sb[n % 2]
            for s in range(4):
                csl = slice(s * 128, (s + 1) * 128)
                mm(P, pM.t[:, 0:8], tri.t[:], dAt.t[:, s, :], True, True, [tri.b, dAt.b], [pM.b])
                mm(P, pM.t[:, 8:16], ones_f.t[:], dAt.t[:, s, :], True, True, [ones_f.b, dAt.b], [pM.b])
                cpy(P, "act", acsS.t[:], pM.t[:, 0:16], [pM.b], [acsS.b])
                ts(P, "dve", nacs.t[:], acsS.t[:, 0:8], -1.0, None, ALU.mult, None, [acsS.b], [nacs.b])
                tt(P, "dve", dte.t[:], acsS.t[:, 8:16], acsS.t[:, 0:8], ALU.subtract, [acsS.b], [dte.b])
                actf(P, Eacs.t[:], acsS.t[:, 0:8], AF.Exp, [acsS.b], [Eacs.b])
                actf(P, cd.t[:], acsS.t[:, 8:16], AF.Exp, [acsS.b], [cd.b])
                actf(P, dte.t[:], dte.t[:], AF.Exp, [dte.b], [dte.b])
                xs = Xtok.t[:, s, :].rearrange("p (h q) -> p h q", q=64)
                tt(P, "dve", xdt.t[:], xs, dtt.t[:, s, :].unsqueeze(2).to_broadcast([128, 8, 64]), ALU.mult, [Xtok.b, dtt.b], [xdt.b])
                tt(P, "dve", xdt2.t[:], xdt.t[:], dte.t[:].unsqueeze(2).to_broadcast([128, 8, 64]), ALU.mult, [xdt.b, dte.b], [xdt2.b])
                tt(P, "dve", R.t[:], tri.t[:].unsqueeze(1).to_broadcast([128, 8, 128]),
                   dAt.t[:, s, :].unsqueeze(2).to_broadcast([128, 8, 128]), ALU.mult, [tri.b, dAt.b], [R.b])
                for half in range(2):
                    mm(P, pD[half].t[:], ones_f.t[:], R.t[:, half * 4:(half + 1) * 4, :].rearrange("p a b -> p (a b)"), True, False,
                       [ones_f.b, R.b], [pD[half].b])
                    mm(P, pD[half].t[:], ident_b.t[:], negmask4.t[:], False, True, [ident_b.b, negmask4.b], [pD[half].b])
                mm(P, pM.t[:, 128:256], BT.t[:, csl], CT.t[:, csl], True, True, [BT.b, CT.b], [pM.b])
                cpy(P, "act", G_sb.t[:], pM.t[:, 128:256], [pM.b], [G_sb.b])
                for h in range(8):
                    actf(P, Dk.t[:, h, :], pD[h // 4].t[:, (h % 4) * 128:(h % 4 + 1) * 128], AF.Exp, [pD[h // 4].b, nacs.b], [Dk.b],
                         bias=nacs.t[:, h:h + 1], scale=1.0)
                tt(P, "dve", M.t[:], Dk.t[:], G_sb.t[:].unsqueeze(1).to_broadcast([128, 8, 128]), ALU.mult, [Dk.b, G_sb.b], [M.b])
                for h in range(8):
                    mm(P, pY.t[:, h * 64:(h + 1) * 64], M.t[:, h, :], xdt.t[:, h, :], True, True, [M.b, xdt.b], [pY.b])
                mm(P, pO.t[:], CT.t[:, csl], prev_bf.t[:], True, True, [CT.b, prev_bf.b], [pO.b])
                mm(P, pS.t[:], Btok.t[:, s, :], xdt2.t[:].rearrange("p a b -> p (a b)"), True, True, [Btok.b, xdt2.b], [pS.b])
                tt(P, "dve", yo.t[:], pO.t[:].rearrange("p (h q) -> p h q", q=64), Eacs.t[:].unsqueeze(2).to_broadcast([128, 8, 64]),
                   ALU.mult, [pO.b, Eacs.b], [yo.b])
                tt(P, "dve", y1.t[:], pY.t[:], yo.t[:].rearrange("p a b -> p (a b)"), ALU.add, [pY.b, yo.b], [y1.b])
                tt(P, "pool", xd.t[:], xs, D_bc.t[:].unsqueeze(2).to_broadcast([128, 8, 64]), ALU.mult, [Xtok.b, D_bc.b], [xd.b])
                tt(P, "pool", y1.t[:], y1.t[:], xd.t[:].rearrange("p a b -> p (a b)"), ALU.add, [y1.b, xd.b], [y1.b])
                tt(P, "pool", y3.t[:], y1.t[:], SZ.t[:, s, :], ALU.mult, [y1.b, SZ.b], [y3.b])
                tt(P, "pool", prevt.t[:], prev.t[:], cd.t[:].unsqueeze(2).to_broadcast([128, 8, 64]), ALU.mult, [prev.b, cd.b], [prevt.b])
                tt(P, "dve", prev.t[:].rearrange("p a b -> p (a b)"), prevt.t[:].rearrange("p a b -> p (a b)"), pS.t[:], ALU.add,
                   [prevt.b, pS.b], [prev.b])
                cpy(P, "pool", prev_bf.t[:], prev.t[:].rearrange("p a b -> p (a b)"), [prev.b], [prev_bf.b])
                actf(P, junk.t[:], y3.t[:], AF.Square, [y3.b], [junk.b, ssq.b], accum_out=ssq.t[:])
                actf(P, lnv.t[:], ssq.t[:], AF.Ln, [ssq.b, epsc.b], [lnv.b], bias=epsc.t[:], scale=1.0 / 512.0)
                actf(P, rstd.t[:], lnv.t[:], AF.Exp, [lnv.b], [rstd.b], scale=-0.5)
                stt(P, y4.t[:], y3.t[:], rstd.t[:, 0:1], nw_bc.t[:], ALU.mult, ALU.mult, [y3.b, rstd.b, nw_bc.b], [y4.b])
                acc = pAB[k % 2]
                k += 1
                for c in range(4):
                    tr(P, acc.t[:, c * 128:(c + 1) * 128], y4.t[:, c * 128:(c + 1) * 128], ident_f.t[:], [y4.b, ident_f.b], [acc.b])
                cpy(P, "act", ysb.t[:, :, csl], acc.t[:].rearrange("p (c l) -> p c l", l=128), [acc.b], [ysb.b])
            ob = ctx.buf("yTout")
            dma(P, "sp", yv[:, g * 4:(g + 1) * 4, n * 512:(n + 1) * 512], ysb.t[:], [ysb.b], [ob])
            ctx.out_bufs.append(ob)


TWO_PI_S5 = 6.2831845
PI_S5 = 3.1415922


def _frac_small(ctx, P, v, shape, name):
    ki = mk(ctx, shape, I32, name + "ki")
    kf = mk(ctx, shape, F32, name + "kf")
    m = mk(ctx, shape, F32, name + "m")
    cpy(P, "dve", ki.t[:], v.t[:], [v.b], [ki.b])
    cpy(P, "dve", kf.t[:], ki.t[:], [ki.b], [kf.b])
    tt(P, "dve", v.t[:], v.t[:], kf.t[:], ALU.subtract, [v.b, kf.b], [v.b])
    ts(P, "dve", m.t[:], v.t[:], 0.0, None, ALU.is_lt, None, [v.b], [m.b])
    tt(P, "dve", v.t[:], v.t[:], m.t[:], ALU.add, [v.b, m.b], [v.b])
    ts(P, "dve", m.t[:], v.t[:], 1.0, None, ALU.is_ge, None, [v.b], [m.b])
    tt(P, "dve", v.t[:], v.t[:], m.t[:], ALU.subtract, [v.b, m.b], [v.b])


def stage_s5_a(ctx, *, hT_tile_view, w_in, s5p, s5b, s5c, s5d, ygT_out, r, consts, nq=4):
    P = ctx.P
    wv = w_in.rearrange("(kc p) n -> p kc n", p=128)
    pU = [mk(ctx, [128, 512], F32, "pU%d" % i, psum=True) for i in range(2)]
    pP = [mk(ctx, [128, 512], F32, "pP%d" % i, psum=True) for i in range(4)]
    pYo = [mk(ctx, [128, 512], F32, "pYo%d" % i, psum=True) for i in range(2)]
    prm = mk(ctx, [128, 48], F32, "prm")
    dma(P, "sp", prm.t[:], s5p, [], [prm.b])
    dsk = mk(ctx, [128, 4], F32, "dsk")
    dma(P, "sp", dsk.t[:], s5d, [], [dsk.b])
    step = mk(ctx, [128, 16], F32, "step")
    rho = mk(ctx, [128, 16], F32, "rho")
    u0 = mk(ctx, [128, 16], F32, "u0")
    uk = mk(ctx, [128, 16, 12], F32, "uk")
    t16 = [mk(ctx, [128, 16], F32, "t16_%d" % i) for i in range(6)]
    cre = mk(ctx, [128, 16], F32, "coef_re")
    cim = mk(ctx, [128, 16], F32, "coef_im")
    negpi = mk(ctx, [128, 1], F32, "negpi")
    memset(P, "pool", negpi.t[:], -PI_S5, [negpi.b])
    lre, lim, lst = prm.t[:, 0:16], prm.t[:, 16:32], prm.t[:, 32:48]
    actf(P, step.t[:], lst, AF.Exp, [prm.b], [step.b])
    tt(P, "dve", t16[0].t[:], lre, step.t[:], ALU.mult, [prm.b, step.b], [t16[0].b])
    actf(P, rho.t[:], t16[0].t[:], AF.Exp, [t16[0].b], [rho.b])
    tt(P, "dve", u0.t[:], lim, step.t[:], ALU.mult, [prm.b, step.b], [u0.b])
    ts(P, "dve", u0.t[:], u0.t[:], float(1.0 / (2.0 * np.pi)), None, ALU.mult, None, [u0.b], [u0.b])
    _frac_small(ctx, P, u0, [128, 16], "f0")
    cpy(P, "dve", uk.t[:, :, 0], u0.t[:], [u0.b], [uk.b])
    mk_ = mk(ctx, [128, 16], F32, "ukm")
    for k in range(1, 12):
        ts(P, "dve", t16[1].t[:], uk.t[:, :, k - 1], 2.0, None, ALU.mult, None, [uk.b], [t16[1].b])
        ts(P, "dve", mk_.t[:], t16[1].t[:], 1.0, None, ALU.is_ge, None, [t16[1].b], [mk_.b])
        tt(P, "dve", uk.t[:, :, k], t16[1].t[:], mk_.t[:], ALU.subtract, [t16[1].b, mk_.b], [uk.b])
    sn, cs = t16[2], t16[3]
    actf(P, sn.t[:], u0.t[:], AF.Sin, [u0.b, negpi.b], [sn.b], bias=negpi.t[:], scale=TWO_PI_S5)
    ts(P, "dve", t16[1].t[:], u0.t[:], 0.25, None, ALU.add, None, [u0.b], [t16[1].b])
    ts(P, "dve", mk_.t[:], t16[1].t[:], 1.0, None, ALU.is_ge, None, [t16[1].b], [mk_.b])
    tt(P, "dve", t16[1].t[:], t16[1].t[:], mk_.t[:], ALU.subtract, [t16[1].b, mk_.b], [t16[1].b])
    actf(P, cs.t[:], t16[1].t[:], AF.Sin, [t16[1].b, negpi.b], [cs.b], bias=negpi.t[:], scale=TWO_PI_S5)
    nre, nim = t16[4], t16[5]
    tt(P, "dve", nre.t[:], rho.t[:], cs.t[:], ALU.mult, [rho.b, cs.b], [nre.b])
    ts(P, "dve", nre.t[:], nre.t[:], -1.0, -1.0, ALU.mult, ALU.add, [nre.b], [nre.b])
    tt(P, "dve", nim.t[:], rho.t[:], sn.t[:], ALU.mult, [rho.b, sn.b], [nim.b])
    ts(P, "dve", nim.t[:], nim.t[:], -1.0, None, ALU.mult, None, [nim.b], [nim.b])
    den = t16[0]
    tt(P, "dve", den.t[:], lre, lre, ALU.mult, [prm.b], [den.b])
    tt(P, "dve", t16[1].t[:], lim, lim, ALU.mult, [prm.b], [t16[1].b])
    tt(P, "dve", den.t[:], den.t[:], t16[1].t[:], ALU.add, [den.b, t16[1].b], [den.b])
    rden = t16[2]
    P.dve(lambda e, o=rden.t[:], i_=den.t[:]: e.reciprocal(out=o, in_=i_), [den.b], [rden.b])
    tt(P, "dve", cre.t[:], nre.t[:], lre, ALU.mult, [nre.b, prm.b], [cre.b])
    tt(P, "dve", t16[1].t[:], nim.t[:], lim, ALU.mult, [nim.b, prm.b], [t16[1].b])
    tt(P, "dve", cre.t[:], cre.t[:], t16[1].t[:], ALU.add, [cre.b, t16[1].b], [cre.b])
    tt(P, "dve", cre.t[:], cre.t[:], rden.t[:], ALU.mult, [cre.b, rden.b], [cre.b])
    tt(P, "dve", cim.t[:], nim.t[:], lre, ALU.mult, [nim.b, prm.b], [cim.b])
    tt(P, "dve", t16[1].t[:], nre.t[:], lim, ALU.mult, [nre.b, prm.b], [t16[1].b])
    tt(P, "dve", cim.t[:], cim.t[:], t16[1].t[:], ALU.subtract, [cim.b, t16[1].b], [cim.b])
    tt(P, "dve", cim.t[:], cim.t[:], rden.t[:], ALU.mult, [cim.b, rden.b], [cim.b])

    Wu = mk(ctx, [128, 16, 512], BF16, "Wu")
    dma(P, "pool", Wu.t[:], wv[:, :, r * 512:(r + 1) * 512], [], [Wu.b])
    hTt = [mk(ctx, [128, 16, 256], BF16, "hT5_%d" % i) for i in range(2)]
    Ubf = mk(ctx, [128, SEQ], BF16, "Ubf")
    SINn = mk(ctx, [128, SEQ], F32, "SINn")
    COSn = mk(ctx, [128, SEQ], F32, "COSn")
    phm = mk(ctx, [128, 2048], F32, "phm")
    Sre = [mk(ctx, [128, SEQ], BF16, "Sre%d" % i) for i in range(4)]
    Sim = [mk(ctx, [128, SEQ], BF16, "Sim%d" % i) for i in range(4)]
    RHO = mk(ctx, [128, 512], F32, "RHO")
    ones512 = mk(ctx, [128, 512], F32, "ones512")
    memset(P, "pool", ones512.t[:], 1.0, [ones512.b])
    bTre = mk(ctx, [128, 128], BF16, "bTre")
    bTim = mk(ctx, [128, 128], BF16, "bTim")
    cfre = mk(ctx, [128, 128], F32, "cfre")
    cfim = mk(ctx, [128, 128], F32, "cfim")
    ctmp = [mk(ctx, [128, 128], F32, "ctmp%d" % i) for i in range(2)]
    Cre = [mk(ctx, [128, 128], BF16, "Cre%d" % i) for i in range(4)]
    Cim = [mk(ctx, [128, 128], BF16, "Cim%d" % i) for i in range(4)]
    pr = mk(ctx, [128, 512], F32, "pr")
    pi_ = mk(ctx, [128, 512], F32, "pi")
    mA = [mk(ctx, [128, 512], F32, "mA%d" % i) for i in range(4)]
    dre = mk(ctx, [128, 512], F32, "dre")
    dim = mk(ctx, [128, 512], F32, "dim")
    rre = [mk(ctx, [128, 512], F32, "rre%d" % i) for i in range(2)]
    rim = [mk(ctx, [128, 512], F32, "rim%d" % i) for i in range(2)]
    yv_ = mk(ctx, [128, 512], F32, "yv")
    ygT_sb = mk(ctx, [128, SEQ], BF16, "ygT_sb")
    hidx = 0
    pidx = 0
    for q in range(nq):
        for n in range(16):
            ht = hTt[hidx % 2]
            hidx += 1
            dma(P, "sp", ht.t[:], hT_tile_view(n), [], [ht.b])
            acc = pU[n % 2]
            for kc in range(16):
                mm(P, acc.t[:, 0:256], Wu.t[:, kc, q * 128:(q + 1) * 128], ht.t[:, kc, :], kc == 0, kc == 15, [Wu.b, ht.b], [acc.b])
            cpy(P, "act", Ubf.t[:, n * 256:(n + 1) * 256], acc.t[:, 0:256], [acc.b], [Ubf.b])
        for jj in range(4):
            j = q * 4 + jj
            dma(P, "pool", bTre.t[:], s5b[0, j], [], [bTre.b])
            dma(P, "pool", bTim.t[:], s5b[1, j], [], [bTim.b])
            dma(P, "sp", cfre.t[:], s5c[0, j], [], [cfre.b])
            dma(P, "sp", cfim.t[:], s5c[1, j], [], [cfim.b])
            ts(P, "dve", ctmp[0].t[:], cfre.t[:], cre.t[:, j:j + 1], None, ALU.mult, None, [cfre.b, cre.b], [ctmp[0].b])
            ts(P, "dve", ctmp[1].t[:], cfim.t[:], cim.t[:, j:j + 1], None, ALU.mult, None, [cfim.b, cim.b], [ctmp[1].b])
            tt(P, "dve", Cre[jj].t[:], ctmp[0].t[:], ctmp[1].t[:], ALU.subtract, [ctmp[0].b, ctmp[1].b], [Cre[jj].b])
            ts(P, "dve", ctmp[0].t[:], cfre.t[:], cim.t[:, j:j + 1], None, ALU.mult, None, [cfre.b, cim.b], [ctmp[0].b])
            ts(P, "dve", ctmp[1].t[:], cfim.t[:], cre.t[:, j:j + 1], None, ALU.mult, None, [cfim.b, cre.b], [ctmp[1].b])
            tt(P, "dve", Cim[jj].t[:], ctmp[0].t[:], ctmp[1].t[:], ALU.add, [ctmp[0].b, ctmp[1].b], [Cim[jj].b])
            memset(P, "pool", COSn.t[:, 0:1], 0.0, [COSn.b])
            for k in range(12):
                w = 1 << k
                ts(P, "dve", phm.t[:, 0:w], COSn.t[:, 0:w], uk.t[:, j, k:k + 1], None, ALU.add, None, [COSn.b, uk.b], [phm.b])
                ts(P, "dve", COSn.t[:, w:2 * w], phm.t[:, 0:w], 1.0, None, ALU.is_ge, None, [phm.b], [COSn.b])
                tt(P, "dve", COSn.t[:, w:2 * w], phm.t[:, 0:w], COSn.t[:, w:2 * w], ALU.subtract, [phm.b, COSn.b], [COSn.b])
            actf(P, SINn.t[:], COSn.t[:], AF.Sin, [COSn.b, negpi.b], [SINn.b], bias=negpi.t[:], scale=TWO_PI_S5)
            for hh in range(2):
                hs = slice(hh * 2048, (hh + 1) * 2048)
                ts(P, "dve", phm.t[:], COSn.t[:, hs], 0.25, None, ALU.add, None, [COSn.b], [phm.b])
                ts(P, "dve", COSn.t[:, hs], phm.t[:], 1.0, None, ALU.is_ge, None, [phm.b], [COSn.b])
                tt(P, "dve", COSn.t[:, hs], phm.t[:], COSn.t[:, hs], ALU.subtract, [phm.b, COSn.b], [COSn.b])
            actf(P, COSn.t[:], COSn.t[:], AF.Sin, [COSn.b, negpi.b], [COSn.b], bias=negpi.t[:], scale=TWO_PI_S5)
            ts(P, "dve", RHO.t[:], ones512.t[:], rho.t[:, j:j + 1], None, ALU.mult, None, [ones512.b, rho.b], [RHO.b])
            for n in range(8):
                tsl = slice(n * 512, (n + 1) * 512)
                p0 = pP[pidx % 4]
                p1 = pP[(pidx + 1) % 4]
                pidx += 2
                mm(P, p0.t[:], bTre.t[:], Ubf.t[:, tsl], True, True, [bTre.b, Ubf.b], [p0.b])
                mm(P, p1.t[:], bTim.t[:], Ubf.t[:, tsl], True, True, [bTim.b, Ubf.b], [p1.b])
                cpy(P, "act", pr.t[:], p0.t[:], [p0.b], [pr.b])
                cpy(P, "act", pi_.t[:], p1.t[:], [p1.b], [pi_.b])
                tt(P, "pool", mA[0].t[:], pr.t[:], COSn.t[:, tsl], ALU.mult, [pr.b, COSn.b], [mA[0].b])
                tt(P, "dve", mA[1].t[:], pi_.t[:], SINn.t[:, tsl], ALU.mult, [pi_.b, SINn.b], [mA[1].b])
                tt(P, "dve", dre.t[:], mA[0].t[:], mA[1].t[:], ALU.add, [mA[0].b, mA[1].b], [dre.b])
                tt(P, "pool", mA[2].t[:], pi_.t[:], COSn.t[:, tsl], ALU.mult, [pi_.b, COSn.b], [mA[2].b])
                tt(P, "dve", mA[3].t[:], pr.t[:], SINn.t[:, tsl], ALU.mult, [pr.b, SINn.b], [mA[3].b])
                tt(P, "pool", dim.t[:], mA[2].t[:], mA[3].t[:], ALU.subtract, [mA[2].b, mA[3].b], [dim.b])
                cur_re, cur_im = rre[n % 2], rim[n % 2]
                prv_re, prv_im = rre[(n + 1) % 2], rim[(n + 1) % 2]
                ini_re = 0.0 if n == 0 else prv_re.t[:, 511:512]
                ini_im = 0.0 if n == 0 else prv_im.t[:, 511:512]
                P.dve(lambda e, o=cur_re.t[:], d0=RHO.t[:], d1=dre.t[:], ini=ini_re: e.tensor_tensor_scan(
                    out=o, data0=d0, data1=d1, initial=ini, op0=ALU.mult, op1=ALU.add), [RHO.b, dre.b, prv_re.b], [cur_re.b])
                P.dve(lambda e, o=cur_im.t[:], d0=RHO.t[:], d1=dim.t[:], ini=ini_im: e.tensor_tensor_scan(
                    out=o, data0=d0, data1=d1, initial=ini, op0=ALU.mult, op1=ALU.add), [RHO.b, dim.b, prv_im.b], [cur_im.b])
                tt(P, "pool", mA[0].t[:], cur_re.t[:], COSn.t[:, tsl], ALU.mult, [cur_re.b, COSn.b], [mA[0].b])
                tt(P, "dve", mA[1].t[:], cur_im.t[:], SINn.t[:, tsl], ALU.mult, [cur_im.b, SINn.b], [mA[1].b])
                tt(P, "dve", Sre[jj].t[:, tsl], mA[0].t[:], mA[1].t[:], ALU.subtract, [mA[0].b, mA[1].b], [Sre[jj].b])
                tt(P, "pool", mA[2].t[:], cur_re.t[:], SINn.t[:, tsl], ALU.mult, [cur_re.b, SINn.b], [mA[2].b])
                stt(P, mA[3].t[:], cur_im.t[:], -1.0, COSn.t[:, tsl], ALU.mult, ALU.mult, [cur_im.b, COSn.b], [mA[3].b])
                tt(P, "pool", Sim[jj].t[:, tsl], mA[3].t[:], mA[2].t[:], ALU.subtract, [mA[3].b, mA[2].b], [Sim[jj].b])
        for n in range(8):
            tsl = slice(n * 512, (n + 1) * 512)
            acc = pYo[n % 2]
            for jj in range(4):
                mm(P, acc.t[:], Cre[jj].t[:], Sre[jj].t[:, tsl], jj == 0, False, [Cre[jj].b, Sre[jj].b], [acc.b])
                mm(P, acc.t[:], Cim[jj].t[:], Sim[jj].t[:, tsl], False, jj == 3, [Cim[jj].b, Sim[jj].b], [acc.b])
            stt(P, yv_.t[:], Ubf.t[:, tsl], dsk.t[:, q:q + 1], acc.t[:], ALU.mult, ALU.add, [Ubf.b, dsk.b, acc.b], [yv_.b])
            actf(P, ygT_sb.t[:, tsl], yv_.t[:], AF.Gelu_apprx_tanh, [yv_.b], [ygT_sb.b])
        ob = ctx.buf("ygout")
        dma(P, "sp", ygT_out[q * 128:(q + 1) * 128, :], ygT_sb.t[:], [ygT_sb.b], [ob])
        ctx.out_bufs.append(ob)


def stage_s5_glu(ctx, *, ygT_view, hT_view, w_z, w_glu, b_glu_v, y2T_out_view):
    P = ctx.P
    wz = w_z.rearrange("(kc p) n -> p kc n", p=128)
    wg = w_glu.rearrange("(kc p) n -> p kc n", p=128)
    yg = mk(ctx, [128, 16, TOK_OWN], BF16, "yg")
    hT = mk(ctx, [128, 16, TOK_OWN], BF16, "hTo")
    dma(P, "sp", yg.t[:], ygT_view, [], [yg.b])
    dma(P, "sp", hT.t[:], hT_view, [], [hT.b])
    bg = mk(ctx, [128, 16], F32, "bg")
    dma(P, "sp", bg.t[:], b_glu_v, [], [bg.b], slow=True)
    Wg_ = [mk(ctx, [128, 16, 128], BF16, "Wgl%d" % i) for i in range(2)]
    Wz_ = [mk(ctx, [128, 16, 128], BF16, "Wzz%d" % i) for i in range(2)]
    ps = [mk(ctx, [128, 512], F32, "psg%d" % i, psum=True) for i in range(4)]
    sig = [mk(ctx, [128, 512], F32, "sig%d" % i) for i in range(2)]
    sz = [mk(ctx, [128, 512], F32, "sz%d" % i) for i in range(2)]
    tmp = mk(ctx, [128, 512], F32, "gtmp")
    y2 = [mk(ctx, [128, TOK_OWN], BF16, "y2_%d" % i) for i in range(2)]
    for c in range(16):
        wgc, wzc = Wg_[c % 2], Wz_[c % 2]
        dma(P, "pool", wgc.t[:], wg[:, :, c * 128:(c + 1) * 128], [], [wgc.b])
        dma(P, "pool", wzc.t[:], wz[:, :, c * 128:(c + 1) * 128], [], [wzc.b])
        y2c = y2[c % 2]
        for n in range(2):
            tsl = slice(n * 512, (n + 1) * 512)
            a0, a1 = ps[n * 2], ps[n * 2 + 1]
            for kc in range(16):
                mm(P, a0.t[:], wgc.t[:, kc, :], yg.t[:, kc, tsl], kc == 0, kc == 15, [wgc.b, yg.b], [a0.b])
            for kc in range(16):
                mm(P, a1.t[:], wzc.t[:, kc, :], hT.t[:, kc, tsl], kc == 0, kc == 15, [wzc.b, hT.b], [a1.b])
        for n in range(2):
            a0 = ps[n * 2]
            actf(P, sig[n].t[:], a0.t[:], AF.Sigmoid, [a0.b, bg.b], [sig[n].b], bias=bg.t[:, c:c + 1], scale=1.0)
        for n in range(2):
            a1 = ps[n * 2 + 1]
            actf(P, sz[n].t[:], a1.t[:], AF.Silu, [a1.b], [sz[n].b])
        for n in range(2):
            tsl = slice(n * 512, (n + 1) * 512)
            tt(P, "dve", tmp.t[:], sig[n].t[:], sz[n].t[:], ALU.mult, [sig[n].b, sz[n].b], [tmp.b])
            tt(P, "dve", y2c.t[:, tsl], tmp.t[:], yg.t[:, c, tsl], ALU.mult, [tmp.b, yg.b], [y2c.b])
        ob = ctx.buf("y2out")
        dma(P, "sp", y2T_out_view[:, c, :], y2c.t[:], [y2c.b], [ob])
        ctx.out_bufs.append(ob)


CN_ALL = ["c_ident_f", "c_ident_b", "c_negmask_b", "c_tri_f", "c_ones_f", "c_ones_b", "c_negmask4_b"]


def _rope_c():
    inv_freq = (10000.0 ** (-np.arange(0, 64, 2, dtype=np.float32) / 64)).astype(np.float32)
    rc = np.zeros((128, 2), np.float32)
    rc[0:32, 0] = inv_freq / (2 * np.pi)
    rc[32:64, 0] = inv_freq / (2 * np.pi)
    rc[0:32, 1] = -1.0
    rc[32:64, 1] = 1.0
    return rc


def s5_host_layout(inp, r):
    g0 = 32 * r
    lam_re = inp["s5_lambda_re"][0][g0:g0 + 32]
    lam_im = inp["s5_lambda_im"][0][g0:g0 + 32]
    lst = np.repeat(inp["s5_log_step"][0][g0:g0 + 32][:, None], 64, axis=1)

    def tl(a):
        return np.ascontiguousarray(a.reshape(16, 2, 64).transpose(1, 2, 0).reshape(128, 16))
    s5p = np.concatenate([tl(lam_re), tl(lam_im), tl(lst)], axis=1).astype(np.float32)
    s5b = np.zeros((2, 16, 128, 128), np.float32)
    s5c = np.zeros((2, 16, 128, 128), np.float32)
    for j in range(16):
        for gl in range(2):
            gloc = 2 * j + gl
            G = g0 + gloc
            gp = gloc % 8
            for ri, (bb, cc) in enumerate(((inp["s5_b_re"][0], inp["s5_c_re"][0]), (inp["s5_b_im"][0], inp["s5_c_im"][0]))):
                s5b[ri, j, gp * 16:(gp + 1) * 16, gl * 64:(gl + 1) * 64] = bb[G].T
                s5c[ri, j, gl * 64:(gl + 1) * 64, gp * 16:(gp + 1) * 16] = cc[G].T
    s5d = np.ascontiguousarray(inp["s5_d"][0][512 * r:512 * (r + 1)].reshape(4, 128).T).astype(np.float32)
    return s5p, s5b, s5c, s5d


def core_weights(inp, r):
    w = {}
    wi = inp["ssd_w_in"][0]
    w["ssd_w"] = np.ascontiguousarray(np.concatenate([
        wi[:, r * 1024:(r + 1) * 1024], wi[:, 4096 + r * 1024:4096 + (r + 1) * 1024],
        wi[:, 8192 + r * 256:8192 + (r + 1) * 256], wi[:, 9216 + r * 256:9216 + (r + 1) * 256],
        wi[:, 10240 + r * 16:10240 + (r + 1) * 16]], axis=1))

    def convsl(a):
        return np.ascontiguousarray(np.concatenate([a[:, r * 1024:(r + 1) * 1024], a[:, 4096 + r * 256:4096 + (r + 1) * 256],
                                                    a[:, 5120 + r * 256:5120 + (r + 1) * 256]], axis=1))
    w["ssd_cw"] = convsl(inp["ssd_conv_w"][0])
    w["ssd_cb"] = convsl(inp["ssd_conv_b"])
    w["ssd_dtb"] = np.ascontiguousarray(inp["ssd_dt_bias"][:, r * 16:(r + 1) * 16])
    w["ssd_alog"] = np.ascontiguousarray(inp["ssd_a_log"][:, r * 16:(r + 1) * 16])
    w["ssd_d"] = np.ascontiguousarray(inp["ssd_d"][:, r * 16:(r + 1) * 16])
    w["ssd_nw"] = np.ascontiguousarray(inp["ssd_norm_w"][:, r * 1024:(r + 1) * 1024])
    wi = inp["fox_w_in"][0]
    w["fox_w"] = np.ascontiguousarray(np.concatenate(
        [wi[:, p * 2048 + r * 512:p * 2048 + (r + 1) * 512] for p in range(4)] + [wi[:, 8192 + 4 * r:8192 + 4 * r + 4]], axis=1))
    w["fox_fb"] = np.ascontiguousarray(inp["fox_f_bias"][:, 4 * r:4 * r + 4])
    wi = inp["mla_w_in"][0]
    w["mla_wz"] = np.ascontiguousarray(wi[:, 1088 + r * 512:1088 + (r + 1) * 512])
    w["mla_wq"] = np.ascontiguousarray(inp["mla_w_q_up"][0][:, r * 768:(r + 1) * 768])
    w["mla_wkv"] = np.ascontiguousarray(inp["mla_w_kv_up"][0][:, r * 1024:(r + 1) * 1024])
    w["s5_wu"] = np.ascontiguousarray(inp["s5_w_in"][0][:, r * 512:(r + 1) * 512])
    w["s5p"], w["s5b"], w["s5c"], w["s5d"] = s5_host_layout(inp, r)
    return w


SSD_OFFS = dict(z0=0, x0=1024, B0=2048, C0=2304, dt0=2560, cx0=0, cB0=1024, cC0=1280)


class Prg:
    def __init__(self):
        self.nc = bass.Bass("TRN2", target_bir_lowering=False)
        self.ins = {}
        self.outs = {}

    def din(self, name, shape, dt):
        self.ins[name] = self.nc.dram_tensor(name, list(shape), dt, kind="ExternalInput").ap()
        return self.ins[name]

    def dout(self, name, shape, dt):
        self.outs[name] = self.nc.dram_tensor(name, list(shape), dt, kind="ExternalOutput").ap()
        return self.outs[name]

    def consts(self, names):
        return declare_consts(self.nc, names)


def run_prog(prg, in_maps):
    res = run_bass_kernel_spmd(prg.nc, in_maps, core_ids=list(range(NCORES)))
    return res.results


def hv_full(hT):
    v = hT.rearrange("(kc p) t -> p kc t", p=128)
    return lambda n: v[:, :, n * 512:(n + 1) * 512]


def build_k0():
    p = Prg()
    x = p.din("x_own", [TOK_OWN, 2048], F32)
    hT = p.dout("hT_own", [2048, TOK_OWN], BF16)
    cd = p.consts(["c_ident_f"])
    with contextlib.ExitStack() as st:
        ctx = Ctx(p.nc, st)
        P = ctx.P
        c = load_consts(ctx, cd, ["c_ident_f"])
        h = mk_h(ctx)
        for t_ in range(8):
            dma(P, "sp", h.t[:, t_, :], x[t_ * 128:(t_ + 1) * 128, :], [], [h.bs[t_]])
        ps = [mk(ctx, [128, 512], F32, "ps%d" % i, psum=True) for i in range(4)]
        ob = emit_transpose_h(ctx, h, hT.rearrange("(fc p) t -> p fc t", p=128), c["c_ident_f"], ps)
        P.emit(final_wait_bufs=[ob])
    return p


def build_mb(F):
    FC = F // 128
    p = Prg()
    yT = p.din("yT_tok", [F, TOK_OWN], BF16)
    h_in = p.din("h_in", [TOK_OWN, 2048], F32)
    w_out = p.din("w_out", [F, 2048], F32)
    lng = p.din("lng", [1, 2048], F32)
    lnb = p.din("lnb", [1, 2048], F32)
    h_out = p.dout("h_out", [TOK_OWN, 2048], F32)
    hT_out = p.dout("hT_own", [2048, TOK_OWN], BF16)
    cd = p.consts(["c_ident_f"])
    with contextlib.ExitStack() as st:
        ctx = Ctx(p.nc, st)
        P = ctx.P
        c = load_consts(ctx, cd, ["c_ident_f"])
        h = mk_h(ctx)
        for t_ in range(8):
            dma(P, "sp", h.t[:, t_, :], h_in[t_ * 128:(t_ + 1) * 128, :], [], [h.bs[t_]])
        ps = [mk(ctx, [128, 512], F32, "ps%d" % i, psum=True) for i in range(4)]
        emit_outproj_ln(ctx, h, yT.rearrange("(fc p) t -> p fc t", p=128), FC, w_out.rearrange("(fc p) n -> p fc n", p=128), lng, lnb, ps)
        obs = []
        for t_ in range(8):
            ob = ctx.buf("ho")
            obs.append(ob)
            dma(P, "sp", h_out[t_ * 128:(t_ + 1) * 128, :], h.t[:, t_, :], [h.bs[t_]], [ob])
        obs.append(emit_transpose_h(ctx, h, hT_out.rearrange("(fc p) t -> p fc t", p=128), c["c_ident_f"], ps))
        P.emit(final_wait_bufs=obs)
    return p


def build_ssd():
    p = Prg()
    hT = p.din("hT_full", [2048, SEQ], BF16)
    w = p.din("ssd_w", [2048, 2576], F32)
    cw = p.din("ssd_cw", [4, 1536], F32)
    cb = p.din("ssd_cb", [1, 1536], F32)
    dtb = p.din("ssd_dtb", [1, 16], F32)
    al = p.din("ssd_alog", [1, 16], F32)
    dd = p.din("ssd_d", [1, 16], F32)
    nw = p.din("ssd_nw", [1, 1024], F32)
    yT = p.dout("yT", [1024, SEQ], BF16)
    names = ["c_ident_f", "c_ident_b", "c_tri_f", "c_ones_f", "c_negmask4_b"]
    cd = p.consts(names)
    with contextlib.ExitStack() as st:
        ctx = Ctx(p.nc, st)
        ctx.out_bufs = []
        c = load_consts(ctx, cd, names)
        stage_ssd(ctx, hT_tile_view=hv_full(hT), w_in=w, conv_w=cw, conv_b=cb, dt_bias=dtb, a_log=al, d_skip=dd, norm_w=nw,
                  yT_out=yT, r=0, consts=c, offs=SSD_OFFS)
        ctx.P.emit(final_wait_bufs=ctx.out_bufs)
    return p


def build_fox():
    p = Prg()
    hT = p.din("hT_full", [2048, SEQ], BF16)
    w = p.din("fox_w", [2048, 2052], F32)
    fb = p.din("fox_fb", [1, 4], F32)
    yT = p.dout("yT", [512, SEQ], BF16)
    names = ["c_ident_b", "c_negmask_b", "c_tri_f", "c_ones_f"]
    cd = p.consts(names)
    with contextlib.ExitStack() as st:
        ctx = Ctx(p.nc, st)
        ctx.out_bufs = []
        c = load_consts(ctx, cd, names)
        stage_fox(ctx, hT_tile_view=hv_full(hT), w_in=w, f_bias_row=fb, yT_out=yT, r=0, consts=c, W=512)
        ctx.P.emit(final_wait_bufs=ctx.out_bufs)
    return p


def build_mla_a():
    p = Prg()
    hT = p.din("hT_own", [2048, TOK_OWN], BF16)
    w = p.din("mla_wl", [2048, 1088], F32)
    pos = p.din("pos_own", [1, TOK_OWN], I32)
    rc = p.din("rope_c", [128, 2], F32)
    latT = p.dout("latT_own", [1152, TOK_OWN], BF16)
    names = ["c_ones_f"]
    cd = p.consts(names)
    with contextlib.ExitStack() as st:
        ctx = Ctx(p.nc, st)
        ctx.out_bufs = []
        c = load_consts(ctx, cd, names)
        rct = mk(ctx, [128, 2], F32, "rope_c")
        dma(ctx.P, "sp", rct.t[:], rc, [], [rct.b])
        hv = hT.rearrange("(kc p) t -> p kc t", p=128)
        lv = latT.rearrange("(c p) t -> p c t", p=128)
        stage_mla_a(ctx, hT_tile_view=lambda n: hv[:, :, n * 512:(n + 1) * 512], ntiles=2, w_in=w, pos_row=pos,
                    latT_out_view=lambda n: lv[:, :, n * 512:(n + 1) * 512], consts=c, rope_c=rct)
        ctx.P.emit(final_wait_bufs=ctx.out_bufs)
    return p


def build_mla_b():
    p = Prg()
    hT = p.din("hT_full", [2048, SEQ], BF16)
    latT = p.din("latT_full", [1152, SEQ], BF16)
    wz = p.din("mla_wz", [2048, 512], F32)
    qn = p.din("mla_qn", [512], F32)
    kvn = p.din("mla_kvn", [512], F32)
    wq = p.din("mla_wq", [512, 768], F32)
    wkv = p.din("mla_wkv", [512, 1024], F32)
    pos = p.din("pos_full", [1, SEQ], I32)
    rc = p.din("rope_c", [128, 2], F32)
    yT = p.dout("yT", [512, SEQ], BF16)
    names = ["c_ident_b", "c_negmask_b"]
    cd = p.consts(names)
    with contextlib.ExitStack() as st:
        ctx = Ctx(p.nc, st)
        ctx.out_bufs = []
        c = load_consts(ctx, cd, names)
        rct = mk(ctx, [128, 2], F32, "rope_c")
        dma(ctx.P, "sp", rct.t[:], rc, [], [rct.b])
        lv = latT.rearrange("(c p) t -> p c t", p=128)
        stage_mla_b(ctx, hT_tile_view=hv_full(hT), latT_tile_view=lambda n: lv[:, 0:8, n * 512:(n + 1) * 512],
                    kpe_view=latT[1024:1088, :], w_z=wz, q_norm_v=qn.rearrange("(c p) -> p c", p=128),
                    kv_norm_v=kvn.rearrange("(c p) -> p c", p=128), w_q_up=wq, w_kv_up=wkv, pos_row=pos, yT_out=yT, r=0,
                    consts=c, rope_c=rct)
        ctx.P.emit(final_wait_bufs=ctx.out_bufs)
    return p


def build_s5a():
    p = Prg()
    hT = p.din("hT_full", [2048, SEQ], BF16)
    wu = p.din("s5_wu", [2048, 512], F32)
    s5p = p.din("s5p", [128, 48], F32)
    s5b = p.din("s5b", [2, 16, 128, 128], F32)
    s5c = p.din("s5c", [2, 16, 128, 128], F32)
    s5d = p.din("s5d", [128, 4], F32)
    ygT = p.dout("yT", [512, SEQ], BF16)
    with contextlib.ExitStack() as st:
        ctx = Ctx(p.nc, st)
        ctx.out_bufs = []
        hv = hT.rearrange("(kc p) t -> p kc t", p=128)
        stage_s5_a(ctx, hT_tile_view=lambda n: hv[:, :, n * 256:(n + 1) * 256], w_in=wu, s5p=s5p, s5b=s5b, s5c=s5c, s5d=s5d,
                   ygT_out=ygT, r=0, consts=None)
        ctx.P.emit(final_wait_bufs=ctx.out_bufs)
    return p


def build_s5b():
    p = Prg()
    ygT = p.din("yT_tok", [2048, TOK_OWN], BF16)
    hT = p.din("hT_own", [2048, TOK_OWN], BF16)
    h_in = p.din("h_in", [TOK_OWN, 2048], F32)
    wz = p.din("s5_wz", [2048, 2048], F32)
    wg = p.din("s5_wglu", [2048, 2048], F32)
    bg = p.din("s5_bglu", [2048], F32)
    w_out = p.din("w_out", [2048, 2048], F32)
    lng = p.din("lng", [1, 2048], F32)
    lnb = p.din("lnb", [1, 2048], F32)
    h_out = p.dout("h_out", [TOK_OWN, 2048], F32)
    y2T = p.nc.dram_tensor("y2T_scratch", [2048, TOK_OWN], BF16).ap()
    with contextlib.ExitStack() as st0:
        with contextlib.ExitStack() as st:
            ctx = Ctx(p.nc, st)
            ctx.out_bufs = []
            stage_s5_glu(ctx, ygT_view=ygT.rearrange("(c p) t -> p c t", p=128), hT_view=hT.rearrange("(c p) t -> p c t", p=128),
                         w_z=wz, w_glu=wg, b_glu_v=bg.rearrange("(c p) -> p c", p=128),
                         y2T_out_view=y2T.rearrange("(c p) t -> p c t", p=128))
            ctx.P.emit(final_wait_bufs=ctx.out_bufs)
        with contextlib.ExitStack() as st:
            ctx = Ctx(p.nc, st)
            P = ctx.P
            h = mk_h(ctx)
            for t_ in range(8):
                dma(P, "sp", h.t[:, t_, :], h_in[t_ * 128:(t_ + 1) * 128, :], [], [h.bs[t_]])
            ps = [mk(ctx, [128, 512], F32, "ps%d" % i, psum=True) for i in range(4)]
            emit_outproj_ln(ctx, h, y2T.rearrange("(fc p) t -> p fc t", p=128), 16, w_out.rearrange("(fc p) n -> p fc n", p=128), lng, lnb, ps)
            obs = []
            for t_ in range(8):
                ob = ctx.buf("ho")
                obs.append(ob)
                dma(P, "sp", h_out[t_ * 128:(t_ + 1) * 128, :], h.t[:, t_, :], [h.bs[t_]], [ob])
            P.emit(final_wait_bufs=obs)
    return p


def _bf(a):
    return np.asarray(a)


def kernel(**inp):
    inp = {k: np.asarray(v) for k, v in inp.items()}
    hc = host_consts()
    rope_c = _rope_c()
    cw = [core_weights(inp, c % GRP) for c in range(NCORES)]

    def cst(names):
        return {n: hc[n] for n in names}

    def tok_slice(c):
        return slice((c % GRP) * TOK_OWN, (c % GRP + 1) * TOK_OWN)

    def gather_tokens(arrs):
        out = []
        for c in range(NCORES):
            b = c // GRP
            out.append(np.concatenate([arrs[b * GRP + rr] for rr in range(GRP)], axis=1))
        return out

    def gather_feats_own_tokens(arrs):
        out = []
        for c in range(NCORES):
            b = c // GRP
            out.append(np.ascontiguousarray(np.concatenate([arrs[b * GRP + rr][:, tok_slice(c)] for rr in range(GRP)], axis=0)))
        return out

    x = inp["x"]
    h_own = [np.ascontiguousarray(x[c // GRP, tok_slice(c)]) for c in range(NCORES)]
    p = build_k0()
    res = run_prog(p, [dict(x_own=h_own[c], **cst(["c_ident_f"])) for c in range(NCORES)])
    hT_own = [res[c]["hT_own"] for c in range(NCORES)]

    def run_mb(F, yT_list, w_out, li):
        pm = build_mb(F)
        ytok = gather_feats_own_tokens(yT_list)
        r_ = run_prog(pm, [dict(yT_tok=ytok[c], h_in=h_own[c], w_out=w_out, lng=inp["ln_g"][li:li + 1], lnb=inp["ln_b"][li:li + 1],
                                **cst(["c_ident_f"])) for c in range(NCORES)])
        return [r_[c]["h_out"] for c in range(NCORES)], [r_[c]["hT_own"] for c in range(NCORES)]

    hT_full = gather_tokens(hT_own)
    p = build_ssd()
    names = ["c_ident_f", "c_ident_b", "c_tri_f", "c_ones_f", "c_negmask4_b"]
    res = run_prog(p, [dict(hT_full=hT_full[c], ssd_w=cw[c]["ssd_w"], ssd_cw=cw[c]["ssd_cw"], ssd_cb=cw[c]["ssd_cb"],
                            ssd_dtb=cw[c]["ssd_dtb"], ssd_alog=cw[c]["ssd_alog"], ssd_d=cw[c]["ssd_d"], ssd_nw=cw[c]["ssd_nw"],
                            **cst(names)) for c in range(NCORES)])
    h_own, hT_own = run_mb(4096, [res[c]["yT"] for c in range(NCORES)], inp["ssd_w_out"][0], 0)
    hT_full = gather_tokens(hT_own)
    p = build_fox()
    names = ["c_ident_b", "c_negmask_b", "c_tri_f", "c_ones_f"]
    res = run_prog(p, [dict(hT_full=hT_full[c], fox_w=cw[c]["fox_w"], fox_fb=cw[c]["fox_fb"], **cst(names)) for c in range(NCORES)])
    h_own, hT_own = run_mb(2048, [res[c]["yT"] for c in range(NCORES)], inp["fox_w_out"][0], 1)
    hT_full = gather_tokens(hT_own)
    pos = inp["positions"].astype(np.int32)
    p = build_mla_a()
    wl = np.ascontiguousarray(inp["mla_w_in"][0][:, 0:1088])
    res = run_prog(p, [dict(hT_own=hT_own[c], mla_wl=wl, pos_own=np.ascontiguousarray(pos[c // GRP:c // GRP + 1, tok_slice(c)]),
                            rope_c=rope_c, **cst(["c_ones_f"])) for c in range(NCORES)])
    lat_full = gather_tokens([res[c]["latT_own"] for c in range(NCORES)])
    p = build_mla_b()
    names = ["c_ident_b", "c_negmask_b"]
    res = run_prog(p, [dict(hT_full=hT_full[c], latT_full=lat_full[c], mla_wz=cw[c]["mla_wz"], mla_qn=inp["mla_q_norm"][0],
                            mla_kvn=inp["mla_kv_norm"][0], mla_wq=cw[c]["mla_wq"], mla_wkv=cw[c]["mla_wkv"],
                            pos_full=np.ascontiguousarray(pos[c // GRP:c // GRP + 1]), rope_c=rope_c, **cst(names)) for c in range(NCORES)])
    h_own, hT_own = run_mb(2048, [res[c]["yT"] for c in range(NCORES)], inp["mla_w_out"][0], 2)
    hT_full = gather_tokens(hT_own)
    p = build_s5a()
    res = run_prog(p, [dict(hT_full=hT_full[c], s5_wu=cw[c]["s5_wu"], s5p=cw[c]["s5p"], s5b=cw[c]["s5b"], s5c=cw[c]["s5c"],
                            s5d=cw[c]["s5d"]) for c in range(NCORES)])
    ytok = gather_feats_own_tokens([res[c]["yT"] for c in range(NCORES)])
    p = build_s5b()
    wz = np.ascontiguousarray(inp["s5_w_in"][0][:, 2048:4096])
    res = run_prog(p, [dict(yT_tok=ytok[c], hT_own=hT_own[c], h_in=h_own[c], s5_wz=wz, s5_wglu=inp["s5_w_glu"][0],
                            s5_bglu=inp["s5_b_glu"][0], w_out=inp["s5_w_out"][0], lng=inp["ln_g"][3:4], lnb=inp["ln_b"][3:4])
                       for c in range(NCORES)])
    out = np.zeros((NB, SEQ, D_MODEL), np.float32)
    for c in range(NCORES):
        out[c // GRP, tok_slice(c)] = res[c]["h_out"]
    return out
```
